# Optimizing a Trainium2 kernel written in Bass

```python
import math, functools
import jax, jax.numpy as jnp
from jax import lax
import numpy as np

D_MODEL = 1024
BATCH = 16
SEQ = 2048
DEPTH = 1

CTX_LEN = 256
GRID_W = 64
ROPE_THETA = 10000.0
NORM_EPS = 1e-6
Q_BLOCK = 128

A_HEADS = 4
A_HEAD_DIM = 64
A_WIDTH = A_HEADS * 2 * A_HEAD_DIM
B_HEADS = 8
B_NOPE = 64
B_ROPE = 32
B_VDIM = 64
B_Q_RANK = 256
B_KV_RANK = 128
B_WIDTH = B_HEADS * B_VDIM
MIX_WIDTH = A_WIDTH + B_WIDTH
IN_WIDTH = 3 * A_WIDTH + B_Q_RANK + B_KV_RANK + B_ROPE
SPLIT_POINTS = (A_WIDTH, 2 * A_WIDTH, 3 * A_WIDTH, 3 * A_WIDTH + B_Q_RANK,
                3 * A_WIDTH + B_Q_RANK + B_KV_RANK)
FFN_HIDDEN = -(-8 * D_MODEL // (3 * 256)) * 256

kernel_name = "hybrid_diffattn_mla_dit_layer"


def _rmsnorm(x, gain=None):
    x32 = x.astype(jnp.float32)
    y = x32 * lax.rsqrt(jnp.mean(x32 * x32, axis=-1, keepdims=True) + NORM_EPS)
    if gain is not None:
        y = y * gain.astype(jnp.float32)
    return y.astype(x.dtype)


def _modulate(h, shift, scale):
    return h * (1 + scale) + shift


def _axial_rope(x, row, col):
    half = x.shape[-1] // 2
    quarter = half // 2
    inv = ROPE_THETA ** (-jnp.arange(quarter, dtype=jnp.float32) / quarter)

    def rot(xa, pos):
        ang = pos.astype(jnp.float32)[:, None] * inv[None, :]
        cos = jnp.concatenate([jnp.cos(ang)] * 2, -1)[None, :, None, :]
        sin = jnp.concatenate([jnp.sin(ang)] * 2, -1)[None, :, None, :]
        xa32 = xa.astype(jnp.float32)
        rotated = jnp.concatenate([-xa32[..., quarter:], xa32[..., :quarter]], -1)
        return (xa32 * cos + rotated * sin).astype(x.dtype)

    return jnp.concatenate([rot(x[..., :half], row), rot(x[..., half:], col)], -1)


def _project(h, w_in, q_a_norm, kv_a_norm, w_q_up, w_kv_up, pos):
    nb, s, _ = h.shape
    qa, ka, va, cq, ckv, krope = jnp.split(h @ w_in, SPLIT_POINTS, axis=-1)
    qa = qa.reshape(nb, s, A_HEADS, 2, A_HEAD_DIM)
    ka = ka.reshape(nb, s, A_HEADS, 2, A_HEAD_DIM)
    q1, q2 = qa[..., 0, :], qa[..., 1, :]
    k1, k2 = ka[..., 0, :], ka[..., 1, :]
    va = va.reshape(nb, s, A_HEADS, 2 * A_HEAD_DIM)
    qb = (_rmsnorm(cq, q_a_norm) @ w_q_up).reshape(nb, s, B_HEADS, B_NOPE + B_ROPE)
    kvb = (_rmsnorm(ckv, kv_a_norm) @ w_kv_up).reshape(nb, s, B_HEADS, B_NOPE + B_VDIM)
    q_nope, q_rope = qb[..., :B_NOPE], qb[..., B_NOPE:]
    k_nope, vb = kvb[..., :B_NOPE], kvb[..., B_NOPE:]
    krope = krope[:, :, None, :]
    if pos is not None:
        row, col = pos
        q1, q2, k1, k2 = (_axial_rope(t, row, col) for t in (q1, q2, k1, k2))
        q_rope = _axial_rope(q_rope, row, col)
        krope = _axial_rope(krope, row, col)
    qb = jnp.concatenate([q_nope, q_rope], -1)
    kb = jnp.concatenate([k_nope, jnp.broadcast_to(krope, (nb, s, B_HEADS, B_ROPE))], -1)
    return q1, q2, k1, k2, va, qb, kb, vb


def _diff_attend(q1, q2, k1, k2, v, lam):
    scale = A_HEAD_DIM ** -0.5
    s1 = jnp.einsum('bqhd,bkhd->bhqk', q1, k1).astype(jnp.float32) * scale
    s2 = jnp.einsum('bqhd,bkhd->bhqk', q2, k2).astype(jnp.float32) * scale
    a = jax.nn.softmax(s1, axis=-1) - lam * jax.nn.softmax(s2, axis=-1)
    return jnp.einsum('bhqk,bkhe->bqhe', a.astype(v.dtype), v)


def _softmax_attend(q, k, v):
    scale = (B_NOPE + B_ROPE) ** -0.5
    s = jnp.einsum('bqhd,bkhd->bhqk', q, k).astype(jnp.float32) * scale
    a = jax.nn.softmax(s, axis=-1)
    return jnp.einsum('bhqk,bkhe->bqhe', a.astype(v.dtype), v)


def _map_query_blocks(fn, *qs):
    nb, s = qs[0].shape[:2]
    n_blk = s // Q_BLOCK
    blocks = tuple(jnp.moveaxis(q.reshape(nb, n_blk, Q_BLOCK, *q.shape[2:]), 1, 0) for q in qs)
    out = lax.map(lambda blk: fn(*blk), blocks)
    out = jnp.moveaxis(out, 0, 1)
    return out.reshape(nb, s, *out.shape[3:])


def _merge_heads(oa, ob, lam_init, diff_subln, w_out):
    nb, s = oa.shape[:2]
    oa = _rmsnorm(oa, diff_subln) * (1 - lam_init)
    cat = jnp.concatenate([oa.reshape(nb, s, A_WIDTH), ob.reshape(nb, s, B_WIDTH)], -1)
    return cat @ w_out


def _swiglu(h, w_ffn_in, w_ffn_out):
    gate, up = jnp.split(h @ w_ffn_in, 2, axis=-1)
    return (jax.nn.silu(gate) * up) @ w_ffn_out


def _layer(x_lat, x_ctx, row, col, c, c_ctx, w_ada, b_ada, w_in, q_a_norm, kv_a_norm,
           w_q_up, w_kv_up, diff_lambda, diff_subln, w_out, w_ffn_in, w_ffn_out,
           layer_idx, update_ctx):
    lam_init = 0.8 - 0.6 * math.exp(-0.3 * layer_idx)
    lp = diff_lambda.astype(jnp.float32)
    lam = jnp.exp(jnp.sum(lp[0] * lp[1])) - jnp.exp(jnp.sum(lp[2] * lp[3])) + lam_init

    sh_a, sc_a, g_a, sh_f, sc_f, g_f = jnp.split((jax.nn.silu(c) @ w_ada + b_ada)[:, None, :], 6, axis=-1)
    csh_a, csc_a, cg_a, csh_f, csc_f, cg_f = jnp.split(jax.nn.silu(c_ctx) @ w_ada + b_ada, 6, axis=-1)

    proj = functools.partial(_project, w_in=w_in, q_a_norm=q_a_norm, kv_a_norm=kv_a_norm,
                             w_q_up=w_q_up, w_kv_up=w_kv_up)
    h_lat = _modulate(_rmsnorm(x_lat), sh_a, sc_a)
    h_ctx = _modulate(_rmsnorm(x_ctx), csh_a, csc_a)
    lq1, lq2, lk1, lk2, lva, lqb, lkb, lvb = proj(h_lat, pos=(row, col))
    cq1, cq2, ck1, ck2, cva, cqb, ckb, cvb = proj(h_ctx, pos=None)

    k1 = jnp.concatenate([ck1, lk1], axis=1)
    k2 = jnp.concatenate([ck2, lk2], axis=1)
    va = jnp.concatenate([cva, lva], axis=1)
    kb = jnp.concatenate([ckb, lkb], axis=1)
    vb = jnp.concatenate([cvb, lvb], axis=1)
    oa = _map_query_blocks(lambda q1, q2: _diff_attend(q1, q2, k1, k2, va, lam), lq1, lq2)
    ob = _map_query_blocks(lambda q: _softmax_attend(q, kb, vb), lqb)
    new_lat = x_lat + g_a * _merge_heads(oa, ob, lam_init, diff_subln, w_out)
    h = _modulate(_rmsnorm(new_lat), sh_f, sc_f)
    new_lat = new_lat + g_f * _swiglu(h, w_ffn_in, w_ffn_out)

    if update_ctx:
        oac = _diff_attend(cq1, cq2, ck1, ck2, cva, lam)
        obc = _softmax_attend(cqb, ckb, cvb)
        new_ctx = x_ctx + cg_a * _merge_heads(oac, obc, lam_init, diff_subln, w_out)
        hc = _modulate(_rmsnorm(new_ctx), csh_f, csc_f)
        new_ctx = new_ctx + cg_f * _swiglu(hc, w_ffn_in, w_ffn_out)
    else:
        new_ctx = x_ctx
    return new_lat, new_ctx


def setup_inputs(seed: int = 0) -> dict:
    key = jax.random.key(seed)
    ks = jax.random.split(key, 17)
    f32 = jnp.float32

    def nrm(k, shape, scale):
        return jax.random.normal(k, shape, f32) * scale

    return {
        "x": nrm(ks[0], (BATCH, SEQ, D_MODEL), 1.0),
        "c": nrm(ks[1], (BATCH, D_MODEL), 1.0),
        "ctx": nrm(ks[2], (BATCH, CTX_LEN, D_MODEL), 1.0),
        "c_ctx": nrm(ks[3], (D_MODEL,), 1.0),
        "w_ada": nrm(ks[4], (DEPTH, D_MODEL, 6 * D_MODEL), 0.5 * D_MODEL ** -0.5),
        "b_ada": nrm(ks[5], (DEPTH, 6 * D_MODEL), 0.01),
        "w_in": nrm(ks[6], (DEPTH, D_MODEL, IN_WIDTH), D_MODEL ** -0.5),
        "q_a_norm": 1.0 + nrm(ks[7], (DEPTH, B_Q_RANK), 0.05),
        "kv_a_norm": 1.0 + nrm(ks[8], (DEPTH, B_KV_RANK), 0.05),
        "w_q_up": nrm(ks[9], (DEPTH, B_Q_RANK, B_HEADS * (B_NOPE + B_ROPE)), B_Q_RANK ** -0.5),
        "w_kv_up": nrm(ks[10], (DEPTH, B_KV_RANK, B_HEADS * (B_NOPE + B_VDIM)), B_KV_RANK ** -0.5),
        "diff_lambda": nrm(ks[11], (DEPTH, 4, A_HEAD_DIM), 0.1),
        "diff_subln": 1.0 + nrm(ks[12], (DEPTH, 2 * A_HEAD_DIM), 0.05),
        "w_out": nrm(ks[13], (DEPTH, MIX_WIDTH, D_MODEL), MIX_WIDTH ** -0.5),
        "w_ffn_in": nrm(ks[14], (DEPTH, D_MODEL, 2 * FFN_HIDDEN), D_MODEL ** -0.5),
        "w_ffn_out": nrm(ks[15], (DEPTH, FFN_HIDDEN, D_MODEL), FFN_HIDDEN ** -0.5),
        "final_norm": 1.0 + nrm(ks[16], (D_MODEL,), 0.05),
    }


def reference(x, c, ctx, c_ctx, w_ada, b_ada, w_in, q_a_norm, kv_a_norm, w_q_up, w_kv_up,
              diff_lambda, diff_subln, w_out, w_ffn_in, w_ffn_out, final_norm):
    seq = x.shape[1]
    rows = seq // GRID_W
    row = jnp.repeat(jnp.arange(rows, dtype=jnp.int32), GRID_W)
    col = jnp.tile(jnp.arange(GRID_W, dtype=jnp.int32), rows)
    x_lat, x_ctx = x, ctx
    for layer in range(DEPTH):
        x_lat, x_ctx = _layer(
            x_lat, x_ctx, row, col, c, c_ctx,
            w_ada[layer], b_ada[layer], w_in[layer], q_a_norm[layer], kv_a_norm[layer],
            w_q_up[layer], w_kv_up[layer], diff_lambda[layer], diff_subln[layer],
            w_out[layer], w_ffn_in[layer], w_ffn_out[layer],
            layer_idx=layer, update_ctx=(layer < DEPTH - 1))
    return _rmsnorm(x_lat, final_norm)
```

```python
import math
from contextlib import ExitStack

import numpy as np
import concourse.bass as bass
import concourse.mybir as mybir
from concourse.bass_utils import run_bass_kernel_spmd

F32 = mybir.dt.float32
BF16 = mybir.dt.bfloat16
AF = mybir.ActivationFunctionType
ALU = mybir.AluOpType

N_CORES = 8
NB = 2
D = 1024
KC = 8
S_LAT = 2048
S_CTX = 256
NKC = 18
FFN_H = 2816
NHC = 22
EPS = 1e-6
LAM_INIT = 0.8 - 0.6 * math.exp(-0.3 * 0)
THETA = 10000.0
GRID_W = 64
import os
PIPE_H = int(os.environ.get('PIPE_H', '1'))
SAME_SYNC = tuple(x for x in os.environ.get('SAME_SYNC', 'dve,act,pool').split(',') if x)
PIPE_O = int(os.environ.get('PIPE_O', '1'))

ENGS = ("pe", "act", "dve", "pool", "sp")
CMP_ENGS = ("pe", "act", "dve", "pool")
DMA_ENGS = ("sp", "act", "pool")
SEM_CHUNK = 6000
N_CMP_SEMS = 6
N_DMA_SEMS = 10


class Buf:
    __slots__ = ("w", "r", "excl")

    def __init__(self):
        self.w = None
        self.r = {}
        self.excl = False


class Op:
    __slots__ = ("eng", "fn", "deps", "is_dma", "need_inc", "sem", "val", "phase")

    def __init__(self, eng, fn, is_dma, phase):
        self.eng = eng
        self.fn = fn
        self.is_dma = is_dma
        self.deps = []
        self.need_inc = False
        self.sem = None
        self.val = None
        self.phase = phase


class Sched:
    def __init__(self, nc, es):
        self.nc = nc
        self.sems = {}
        for e in CMP_ENGS:
            for i in range(N_CMP_SEMS):
                self.sems[("cmp", e, i)] = es.enter_context(nc.semaphore("c_%s_%d" % (e, i)))
        for e in DMA_ENGS:
            for i in range(N_DMA_SEMS):
                self.sems[("dma", e, i)] = es.enter_context(nc.semaphore("d_%s_%d" % (e, i)))
        self.cnt = {e: 0 for e in ENGS}
        self.dcnt = {e: 0 for e in ENGS}
        self.seen = {e: {} for e in ENGS}
        self.phase = 0
        self.rec = None
        self.pad_on = False
        self.bg = []
        self.ops = {e: [] for e in ENGS}
        self.all = []

    def rec_begin(self):
        self.rec = [[]]

    def rec_yield(self):
        if self.rec is not None and self.rec[-1]:
            self.rec.append([])

    def rec_pad(self, n=1):
        if self.rec is not None:
            if self.rec[-1]:
                self.rec.append([])
            if not self.pad_on:
                return
            for _ in range(n):
                self.rec.append([None])
                self.rec.append([])

    def rec_end(self):
        r = [g for g in self.rec if g]
        self.rec = None
        return r

    def replay(self, group):
        for args in group:
            if args is not None:
                self.add(*args)

    def tick(self, n=1):
        if self.rec is not None:
            return
        while n > 0 and self.bg:
            self.replay(self.bg.pop(0))
            n -= 1

    def flush_bg(self):
        self.tick(1 << 30)

    def add(self, eng, fn, reads=(), writes=(), dma=False):
        if self.rec is not None:
            self.rec[-1].append((eng, fn, tuple(reads), tuple(writes), dma))
            return None
        op = Op(eng, fn, dma, self.phase)
        seen = set()
        deps = op.deps
        ph = self.phase
        for b in reads:
            d = b.w
            if d is not None and d.phase == ph and id(d) not in seen:
                seen.add(id(d))
                deps.append(d)
            if b.excl:
                for d in b.r.values():
                    if d.eng != eng and d.phase == ph and id(d) not in seen:
                        seen.add(id(d))
                        deps.append(d)
        for b in writes:
            d = b.w
            if d is not None and d.phase == ph and id(d) not in seen:
                seen.add(id(d))
                deps.append(d)
            for d in b.r.values():
                if d.phase == ph and id(d) not in seen:
                    seen.add(id(d))
                    deps.append(d)
        for b in writes:
            b.w = op
            b.r = {}
        for b in reads:
            if b.w is not op:
                b.r[id(op) if dma else eng] = op
        self.all.append(op)
        self.ops[eng].append(op)
        return op

    @staticmethod
    def _needs_wait(op, d):
        if d.is_dma:
            return True
        if d.eng != op.eng:
            return True
        if op.eng == "pe":
            return False
        if op.is_dma:
            return True
        return op.eng in SAME_SYNC

    def emit(self):
        nc = self.nc
        for op in self.all:
            for d in op.deps:
                if self._needs_wait(op, d):
                    d.need_inc = True
        for e in ENGS:
            last = None
            for op in self.ops[e]:
                if op.is_dma:
                    op.need_inc = True
                else:
                    last = op
            if last is not None:
                last.need_inc = True
        for op in self.all:
            if not op.need_inc:
                continue
            e = op.eng
            if op.is_dma:
                k = self.dcnt[e]
                op.sem = ("dma", e, k % N_DMA_SEMS)
                op.val = 16 * (k // N_DMA_SEMS + 1)
                self.dcnt[e] = k + 1
            else:
                k = self.cnt[e]
                assert k < SEM_CHUNK * N_CMP_SEMS, "too many semaphore increments on " + e
                op.sem = ("cmp", e, k // SEM_CHUNK)
                op.val = k % SEM_CHUNK + 1
                self.cnt[e] = k + 1
        targets = []
        for e in CMP_ENGS:
            k = self.cnt[e]
            if k > 0:
                targets.append((("cmp", e, (k - 1) // SEM_CHUNK), (k - 1) % SEM_CHUNK + 1))
        for e in DMA_ENGS:
            k = self.dcnt[e]
            for i in range(N_DMA_SEMS):
                n = (k - i + N_DMA_SEMS - 1) // N_DMA_SEMS if k > i else 0
                if n > 0:
                    targets.append((("dma", e, i), 16 * n))
        sems = self.sems
        sched = self

        def run(e, eng):
            seen = sched.seen[e]
            for op in sched.ops[e]:
                waits = {}
                for d in op.deps:
                    if not sched._needs_wait(op, d):
                        continue
                    if waits.get(d.sem, 0) < d.val:
                        waits[d.sem] = d.val
                for k, v in waits.items():
                    if seen.get(k, 0) >= v:
                        continue
                    seen[k] = v
                    eng.wait_ge(sems[k], v)
                ins = op.fn(eng)
                if op.need_inc:
                    ins.then_inc(sems[op.sem], 16 if op.is_dma else 1)
            for k, v in targets:
                if seen.get(k, 0) >= v:
                    continue
                seen[k] = v
                eng.wait_ge(sems[k], v)

        with nc.Block() as block:
            @block.tensor
            def _(eng):
                run("pe", eng)

            @block.scalar
            def _(eng):
                run("act", eng)

            @block.vector
            def _(eng):
                run("dve", eng)

            @block.gpsimd
            def _(eng):
                run("pool", eng)

            @block.sync
            def _(eng):
                run("sp", eng)

        self.phase += 1
        self.ops = {e: [] for e in ENGS}
        self.all = []


class T:
    __slots__ = ("t", "b", "rb")

    def __init__(self, t, nparts=1):
        self.t = t
        self.rb = [Buf() for _ in range(nparts)]
        self.b = self.rb[0]


class Ring:
    def __init__(self, items):
        self.items = items
        self.i = 0

    def next(self):
        it = self.items[self.i % len(self.items)]
        self.i += 1
        return it


def build_program():
    nc = bass.Bass("TRN2", target_bir_lowering=False)

    def din(name, shape):
        return nc.dram_tensor(name, list(shape), F32, kind="ExternalInput").ap()

    x_d = din("x", [NB, S_LAT, D])
    ctx_d = din("ctx", [NB, S_CTX, D])
    smalls_d = din("smalls", [128, 320])
    bigs_d = din("bigs", [128, 3072])
    wada_d = din("w_ada", [D, 6 * D])
    wD_d = din("w_in_d", [D, 2560])
    wM_d = din("w_in_m", [D, 448])
    wqup_d = din("w_qup", [256, 768])
    wqupp_d = din("w_qupp", [256, 256])
    wkvk_d = din("w_kvk", [128, 512])
    wkvv_d = din("w_kvv", [128, 512])
    wout_d = din("w_out", [D, D])
    wfi_d = din("w_ffn_in", [D, 2 * FFN_H])
    wfo_d = din("w_ffn_out", [FFN_H, D])
    tabA_d = din("tab_a", [128, 2, S_LAT])
    tabB_d = din("tab_b", [128, 2, S_LAT])
    y_d = nc.dram_tensor("y", [NB, S_LAT, D], F32, kind="ExternalOutput").ap()

    with ExitStack() as es:
        S = Sched(nc, es)

        uid = [0]

        def sb(st, name, shape, dt):
            uid[0] += 1
            return st.enter_context(nc.sbuf_tensor("%s_u%d" % (name, uid[0]), list(shape), dt))

        def ring(st, name, shape, dt, n, nparts=1):
            return Ring([T(sb(st, "%s%d" % (name, i), shape, dt), nparts) for i in range(n)])

        banks = [T(es.enter_context(nc.psum_tensor("bk%d" % i, [128, 512], F32))) for i in range(8)]
        for bk in banks:
            bk.b.excl = True

        def bf(bank):
            return bank.t[:].bitcast(BF16)

        ident = T(sb(es, "ident", [128, 128], BF16))
        identf = T(sb(es, "identf", [128, 128], F32))
        ones_bf = T(sb(es, "ones_bf", [128, 128], BF16))
        ones_f = T(sb(es, "ones_f", [128, 128], F32))
        neghalf = T(sb(es, "neghalf", [128, 512], F32))
        smalls = T(sb(es, "smalls", [128, 320], F32))
        fnorm_bc = T(sb(es, "fnorm_bc", [128, D], F32))
        modfm = T(sb(es, "modfm", [128, 4, KC, 3], F32))
        G = T(sb(es, "G", [128, NB, 2, D], F32))
        nlam = T(sb(es, "nlam", [128, 1], F32))
        subln = T(sb(es, "subln", [128, 1], F32))
        C_CT, C_BFM, C_DL, C_SUB, C_QN, C_KVN = 0, 24, 56, 312, 313, 315

        with ExitStack() as ph:
            wada = T(sb(ph, "wada", [128, KC, 6 * D], BF16), 6)
            bada_g = T(sb(ph, "bada_g", [128, 2 * D], F32))
            scf = T(sb(ph, "scf", [128, 24], F32))
            scT = T(sb(ph, "scT", [128, 24], BF16))
            scB = T(sb(ph, "scB", [128, NB, KC, 128], BF16))
            lt = T(sb(ph, "lt", [128, 128], F32))
            ls = T(sb(ph, "ls", [128, 4], F32))

            S.add("sp", lambda e: e.dma_start(out=smalls.t[:], in_=smalls_d), writes=[smalls.b], dma=True)
            S.add("sp", lambda e: e.dma_start(out=bada_g.t[:], in_=bigs_d[:, 0:2 * D]), writes=[bada_g.b], dma=True)
            S.add("sp", lambda e: e.dma_start(out=fnorm_bc.t[:], in_=bigs_d[:, 2 * D:3 * D]), writes=[fnorm_bc.b], dma=True)
            wada_src = wada_d.rearrange("(c p) n -> p c n", p=128)
            for kind in (0, 1, 2, 3, 4, 5):
                S.add("pool", (lambda kind: lambda e: e.dma_start(
                    out=wada.t[:, :, kind * D:(kind + 1) * D], in_=wada_src[:, :, kind * D:(kind + 1) * D]))(kind),
                    writes=[wada.rb[kind]], dma=True)
            S.add("dve", lambda e: e.memset(identf.t[:], 0.0), writes=[identf.b])
            S.add("pool", lambda e: e.affine_select(out=identf.t[:], in_=identf.t[:], pattern=[[-1, 128]],
                                                    compare_op=ALU.not_equal, fill=1.0, base=0, channel_multiplier=1),
                  reads=[identf.b], writes=[identf.b])
            S.add("dve", lambda e: e.tensor_copy(out=ident.t[:], in_=identf.t[:]), reads=[identf.b], writes=[ident.b])
            S.add("dve", lambda e: e.memset(ones_bf.t[:], 1.0), writes=[ones_bf.b])
            S.add("dve", lambda e: e.memset(ones_f.t[:], 1.0), writes=[ones_f.b])
            S.add("dve", lambda e: e.memset(neghalf.t[:], -0.5), writes=[neghalf.b])
            S.add("dve", lambda e: e.tensor_tensor(out=lt.t[:, 0:64], in0=smalls.t[:, C_DL:C_DL + 64],
                                                   in1=smalls.t[:, C_DL + 64:C_DL + 128], op=ALU.mult),
                  reads=[smalls.b], writes=[lt.b])
            S.add("dve", lambda e: e.tensor_tensor(out=lt.t[:, 64:128], in0=smalls.t[:, C_DL + 128:C_DL + 192],
                                                   in1=smalls.t[:, C_DL + 192:C_DL + 256], op=ALU.mult),
                  reads=[smalls.b], writes=[lt.b])
            S.add("dve", lambda e: e.tensor_reduce(out=ls.t[:, 0:2], in_=lt.t[:].rearrange("p (a b) -> p a b", a=2),
                                                   axis=mybir.AxisListType.X, op=ALU.add),
                  reads=[lt.b], writes=[ls.b])
            S.add("act", lambda e: e.activation(out=ls.t[:, 2:4], in_=ls.t[:, 0:2], func=AF.Exp),
                  reads=[ls.b], writes=[ls.b])
            S.add("dve", lambda e: e.tensor_tensor(out=nlam.t[:], in0=ls.t[:, 3:4], in1=ls.t[:, 2:3], op=ALU.subtract),
                  reads=[ls.b], writes=[nlam.b])
            S.add("dve", lambda e: e.tensor_scalar(out=nlam.t[:], in0=nlam.t[:], scalar1=-LAM_INIT, scalar2=None,
                                                   op0=ALU.add), reads=[nlam.b], writes=[nlam.b])
            S.add("dve", lambda e: e.tensor_scalar(out=subln.t[:], in0=smalls.t[:, C_SUB:C_SUB + 1],
                                                   scalar1=1.0 - LAM_INIT, scalar2=None, op0=ALU.mult),
                  reads=[smalls.b], writes=[subln.b])
            S.add("act", lambda e: e.activation(out=scf.t[:], in_=smalls.t[:, C_CT:C_CT + 24], func=AF.Silu),
                  reads=[smalls.b], writes=[scf.b])
            S.add("dve", lambda e: e.tensor_copy(out=scT.t[:], in_=scf.t[:]), reads=[scf.b], writes=[scT.b])
            for j in range(NB):
                for c in range(KC):
                    S.add("dve", (lambda j, c: lambda e: e.tensor_scalar(
                        out=scB.t[:, j, c, :], in0=ones_bf.t[:], scalar1=scf.t[:, c * 3 + j:c * 3 + j + 1],
                        scalar2=None, op0=ALU.mult))(j, c), reads=[ones_bf.b, scf.b], writes=[scB.b])
            fm_bank = banks[0]
            kinds_fm = (0, 1, 3, 4)
            first = True
            for ki, kind in enumerate(kinds_fm):
                for m in range(KC):
                    col = (ki * KC + m) * 4
                    for c in range(KC):
                        S.add("pe", (lambda kind, m, c, col: lambda e: e.matmul(
                            fm_bank.t[:, col:col + 3],
                            lhsT=wada.t[:, c, kind * D + m * 128:kind * D + (m + 1) * 128],
                            rhs=scT.t[:, c * 3:c * 3 + 3], start=(c == 0), stop=(c == KC - 1)))(kind, m, c, col),
                            reads=[wada.rb[kind], scT.b], writes=[fm_bank.b])
            for j in range(3):
                S.add("dve", (lambda j: lambda e: e.tensor_tensor(
                    out=modfm.t[:, :, :, j].rearrange("p a b -> p (a b)"),
                    in0=fm_bank.t[:, 0:128].rearrange("p (a b) -> p a b", b=4)[:, :, j],
                    in1=smalls.t[:, C_BFM:C_BFM + 32], op=ALU.add))(j),
                    reads=[fm_bank.b, smalls.b], writes=[modfm.b])
            for ki in (1, 3):
                S.add("dve", (lambda ki: lambda e: e.tensor_scalar(
                    out=modfm.t[:, ki, :, :], in0=modfm.t[:, ki, :, :], scalar1=1.0, scalar2=None,
                    op0=ALU.add))(ki), reads=[modfm.b], writes=[modfm.b])
            bi = 1
            for j in range(NB):
                for gi, kind in enumerate((2, 5)):
                    for half in range(2):
                        bk = banks[1 + (bi % 7)]
                        bi += 1
                        for c in range(KC):
                            S.add("pe", (lambda j, kind, half, c, bk: lambda e: e.matmul(
                                bk.t[:, :], lhsT=scB.t[:, j, c, :],
                                rhs=wada.t[:, c, kind * D + half * 512:kind * D + (half + 1) * 512],
                                start=(c == 0), stop=(c == KC - 1)))(j, kind, half, c, bk),
                                reads=[scB.b, wada.rb[kind]], writes=[bk.b])
                        S.add("dve", (lambda j, gi, half, bk: lambda e: e.tensor_tensor(
                            out=G.t[:, j, gi, half * 512:(half + 1) * 512], in0=bk.t[:, :],
                            in1=bada_g.t[:, gi * D + half * 512:gi * D + (half + 1) * 512], op=ALU.add))(j, gi, half, bk),
                            reads=[bk.b, bada_g.b], writes=[G.b])
            S.emit()

        def h_block(st_pools, src_ap, ntiles, hT, jm, kind_sc, kind_sh, tpool, evac_engs, keep=None):
            xpool, xnpool, junk, st = st_pools

            def load_a(t):
                xt = xpool.next()
                S.add("sp", (lambda xt, t: lambda e: e.dma_start(out=xt.t[:], in_=src_ap[t * 128:(t + 1) * 128, :]))(xt, t),
                      writes=[xt.b], dma=True)
                S.rec_pad(2)
                return norm_a(st_pools, xt)

            if PIPE_H:
                a = load_a(0)
                for t in range(ntiles):
                    nxt = load_a(t + 1) if t + 1 < ntiles else None
                    trans_b(a, t, hT, jm, kind_sc, kind_sh, tpool, evac_engs)
                    a = nxt
            else:
                for t in range(ntiles):
                    a = load_a(t)
                    trans_b(a, t, hT, jm, kind_sc, kind_sh, tpool, evac_engs)

        XN_ENG = ["act"]

        def norm_a(st_pools, xt):
            xpool, xnpool, junk, st = st_pools
            s = st.next()
            xn = xnpool.next()
            S.add("dve", lambda e: e.scalar_tensor_tensor(out=junk.t[:], in0=xt.t[:], scalar=1.0, in1=xt.t[:],
                                                          op0=ALU.mult, op1=ALU.mult, accum_out=s.t[:, 0:1]),
                  reads=[xt.b], writes=[s.b])
            S.add("dve", lambda e: e.tensor_scalar(out=s.t[:, 1:2], in0=s.t[:, 0:1], scalar1=1.0 / D, scalar2=EPS,
                                                   op0=ALU.mult, op1=ALU.add), reads=[s.b], writes=[s.b])
            S.rec_yield()
            S.add("pool", lambda e: e.tensor_tensor(out=s.t[:, 2:3], in0=s.t[:, 1:2], in1=neghalf.t[:, 0:1], op=ALU.pow),
                  reads=[s.b], writes=[s.b])
            S.rec_yield()
            if XN_ENG[0] == "act":
                S.add("act", lambda e: e.activation(out=xn.t[:], in_=xt.t[:], func=AF.Copy, scale=s.t[:, 2:3]),
                      reads=[xt.b, s.b], writes=[xn.b])
            else:
                S.add("dve", lambda e: e.tensor_scalar(out=xn.t[:], in0=xt.t[:], scalar1=s.t[:, 2:3], scalar2=None,
                                                       op0=ALU.mult), reads=[xt.b, s.b], writes=[xn.b])
            S.rec_pad(3)
            return xn

        def trans_b(xn, t, hT, jm, kind_sc, kind_sh, tpool, evac_engs):
            tbs = tpool.next()
            nb_ = len(tbs)
            per = KC // nb_
            for c in range(KC):
                tb = tbs[c // per]
                tbf = bf(tb)
                cc = c % per
                S.add("pe", (lambda c, cc, tbf: lambda e: e.transpose(tbf[:, cc * 128:(cc + 1) * 128],
                                                                     xn.t[:, c * 128:(c + 1) * 128], ident.t[:]))(c, cc, tbf),
                      reads=[xn.b, ident.b], writes=[tb.b])
            S.rec_pad(1)
            for c in range(KC):
                tb = tbs[c // per]
                tbf = bf(tb)
                cc = c % per
                eng = "dve" if (nb_ == 1 or c // per == 0) else "act"
                sc_ap = modfm.t[:, kind_sc, c, jm:jm + 1]
                sh_ap = modfm.t[:, kind_sh, c, jm:jm + 1]
                if eng == "act":
                    S.add("act", (lambda c, cc, tbf, sc_ap, sh_ap: lambda e: e.activation(
                        out=hT.t[:, c, t * 128:(t + 1) * 128], in_=tbf[:, cc * 128:(cc + 1) * 128], func=AF.Identity,
                        scale=sc_ap, bias=sh_ap))(c, cc, tbf, sc_ap, sh_ap), reads=[tb.b, modfm.b], writes=[hT.rb[c]])
                else:
                    S.add("dve", (lambda c, cc, tbf, sc_ap, sh_ap: lambda e: e.tensor_scalar(
                        out=hT.t[:, c, t * 128:(t + 1) * 128], in0=tbf[:, cc * 128:(cc + 1) * 128], scalar1=sc_ap,
                        scalar2=sh_ap, op0=ALU.mult, op1=ALU.add))(c, cc, tbf, sc_ap, sh_ap),
                        reads=[tb.b, modfm.b], writes=[hT.rb[c]])
            S.rec_yield()

        def proj_fm(bank, w, col0, M, hT, N, nk=KC, rhs3=True):
            for c in range(nk):
                S.add("pe", (lambda c: lambda e: e.matmul(
                    bank.t[0:M, 0:N], lhsT=w.t[:, c, col0:col0 + M], rhs=hT.t[:, c, 0:N],
                    start=(c == 0), stop=(c == nk - 1)))(c), reads=w.rb + [hT.rb[c]], writes=[bank.b])
            S.rec_yield()
            S.tick(2)

        def attention(groups, spool, obanks, ppool, bg=None, every=2):
            LA = len(spool.items) - 1
            bg = list(bg) if bg else []
            it_no = [0]
            its = []
            for g in groups:
                for kc in range(NKC):
                    its.append((g, kc))
            pend = []

            def do_av(g, kc, pt):
                for t in range(4):
                    S.add("pe", (lambda t: lambda e: e.matmul(
                        obanks[t].t[:, 0:g["ncols"]], lhsT=pt.t[:, t * 128:(t + 1) * 128], rhs=g["v"](kc),
                        start=(kc == 0), stop=(kc == NKC - 1)))(t),
                        reads=[pt.b] + g["vreads"], writes=[obanks[t].b])
                if kc == NKC - 1:
                    g["post"]()

            for (g, kc) in its:
                sbk = spool.next()
                S.add("pe", (lambda g, kc, sbk: lambda e: e.matmul(
                    sbk.t[:, :], lhsT=g["kT"](kc), rhs=g["qT"], start=True, stop=True))(g, kc, sbk),
                    reads=g["qkreads"], writes=[sbk.b])
                pt = ppool.next()
                S.add("act", (lambda g, sbk, pt: lambda e: e.activation(
                    out=pt.t[:], in_=sbk.t[:, :], func=AF.Exp, scale=g["scale"]))(g, sbk, pt),
                    reads=[sbk.b], writes=[pt.b])
                pend.append((g, kc, pt))
                if len(pend) > LA:
                    do_av(*pend.pop(0))
                it_no[0] += 1
                if bg and it_no[0] % every == 0:
                    S.replay(bg.pop(0))
            while pend:
                do_av(*pend.pop(0))
            while bg:
                S.replay(bg.pop(0))

        for j in range(NB):
            with ExitStack() as pb:
                catT = T(sb(pb, "catT", [128, KC, S_LAT], BF16))

                with ExitStack() as pd:
                    kAT = T(sb(pd, "kAT", [128, 4, NKC * 128], BF16))
                    vA = T(sb(pd, "vA", [128, NKC, 4 * 129], BF16))

                    with ExitStack() as ph:
                        w = T(sb(ph, "wdk", [128, KC, 1536], BF16), 3)
                        hpool = ring(ph, "hT", [128, KC, 512], BF16, 2, KC)
                        xpool = ring(ph, "xt", [128, D], F32, 2)
                        xnpool = ring(ph, "xn", [128, D], BF16, 2)
                        junk = T(sb(ph, "junk", [128, D], BF16))
                        stp = ring(ph, "st", [128, 4], F32, 4)
                        tabp = ring(ph, "tab", [128, 2, 512], F32, 2)
                        t1p = ring(ph, "t1", [128, 512], F32, 2)
                        t2p = ring(ph, "t2", [128, 512], F32, 2)
                        pools = (xpool, xnpool, junk, stp)
                        tpool = Ring([(banks[4], banks[5]), (banks[6], banks[7])])
                        ppool = Ring(banks[0:4])
                        wsrc = wD_d.rearrange("(c p) n -> p c n", p=128)
                        for part in range(3):
                            S.add("pool", (lambda part: lambda e: e.dma_start(
                                out=w.t[:, :, part * 512:(part + 1) * 512], in_=wsrc[:, :, part * 512:(part + 1) * 512]))(part),
                                writes=[w.rb[part]], dma=True)
                        S.add("pool", lambda e: e.memset(vA.t[:], 1.0), writes=[vA.b])
                        def hb_d(blk):
                            lat = blk > 0
                            ntiles = 4 if lat else 2
                            src = x_d[j, (blk - 1) * 512:blk * 512, :] if lat else ctx_d[j]
                            hT = hpool.next()
                            tab = None
                            if lat:
                                tab = tabp.next()
                                S.add("sp", (lambda tab, blk: lambda e: e.dma_start(
                                    out=tab.t[:], in_=tabA_d[:, :, (blk - 1) * 512:blk * 512]))(tab, blk),
                                    writes=[tab.b], dma=True)
                            h_block(pools, src, ntiles, hT, j if lat else 2, 1, 0, tpool, ("dve", "act"))
                            return hT, tab

                        cur = hb_d(0)
                        for blk in range(5):
                            lat = blk > 0
                            ntiles = 4 if lat else 2
                            N = ntiles * 128
                            key0 = 0 if not lat else S_CTX + (blk - 1) * 512
                            hT, tab = cur
                            if blk < 4:
                                S.rec_begin()
                                cur = hb_d(blk + 1)
                                S.bg = S.rec_end()
                            for m in range(4):
                                ba = ppool.next()
                                proj_fm(ba, w, m * 128, 128, hT, N)
                                if lat:
                                    bb = ppool.next()
                                    proj_fm(bb, w, 512 + m * 128, 128, hT, N)
                                    t1 = t1p.next()
                                    t2 = t2p.next()
                                    S.add("dve", (lambda ba, t1, tab: lambda e: e.tensor_tensor(
                                        out=t1.t[:], in0=ba.t[:, :], in1=tab.t[:, 0, :], op=ALU.mult))(ba, t1, tab),
                                        reads=[ba.b, tab.b], writes=[t1.b])
                                    S.add("dve", (lambda bb, t2, tab: lambda e: e.tensor_tensor(
                                        out=t2.t[:], in0=bb.t[:, :], in1=tab.t[:, 1, :], op=ALU.mult))(bb, t2, tab),
                                        reads=[bb.b, tab.b], writes=[t2.b])
                                    S.add("pool", (lambda m, t1, t2, key0: lambda e: e.tensor_tensor(
                                        out=kAT.t[:, m, key0:key0 + 512], in0=t1.t[:], in1=t2.t[:], op=ALU.add))(m, t1, t2, key0),
                                        reads=[t1.b, t2.b], writes=[kAT.b])
                                else:
                                    S.add("act", (lambda m, ba, N: lambda e: e.activation(
                                        out=kAT.t[:, m, 0:N], in_=ba.t[:, 0:N], func=AF.Copy))(m, ba, N),
                                        reads=[ba.b], writes=[kAT.b])
                            for t in range(ntiles):
                                bv = ppool.next()
                                for c in range(KC):
                                    S.add("pe", (lambda t, c, bv, hT: lambda e: e.matmul(
                                        bv.t[:, :], lhsT=hT.t[:, c, t * 128:(t + 1) * 128], rhs=w.t[:, c, 1024:1536],
                                        start=(c == 0), stop=(c == KC - 1)))(t, c, bv, hT),
                                        reads=[hT.rb[c]] + w.rb, writes=[bv.b])
                                kc = key0 // 128 + t
                                S.add("act", (lambda kc, bv: lambda e: e.activation(
                                    out=vA.t[:, kc, :].rearrange("p (h e) -> p h e", e=129)[:, :, 0:128],
                                    in_=bv.t[:, :].rearrange("p (h e) -> p h e", e=128), func=AF.Copy))(kc, bv),
                                    reads=[bv.b], writes=[vA.b])
                                S.tick(2)
                            S.flush_bg()
                        S.emit()

                    with ExitStack() as ph:
                        w = T(sb(ph, "wdq", [128, KC, 1024], BF16), 2)
                        hpool = ring(ph, "hT", [128, KC, 512], BF16, 1, KC)
                        xpool = ring(ph, "xt", [128, D], F32, 2)
                        xnpool = ring(ph, "xn", [128, D], BF16, 2)
                        junk = T(sb(ph, "junk", [128, D], BF16))
                        stp = ring(ph, "st", [128, 4], F32, 4)
                        tabp = ring(ph, "tab", [128, 2, 512], F32, 2)
                        t1p = ring(ph, "t1", [128, 512], F32, 2)
                        t2p = ring(ph, "t2", [128, 512], F32, 2)
                        qpool = ring(ph, "qAT", [128, 8, 512], BF16, 2)
                        for qz in qpool.items:
                            S.add("pool", (lambda qz: lambda e: e.memset(qz.t[:], 0.0))(qz), writes=[qz.b])
                        ptp = ring(ph, "pt", [128, 512], BF16, 4)
                        orawp = ring(ph, "oraw", [128, 4, 132], F32, 3)
                        rp = ring(ph, "rr", [128, 8], F32, 4)
                        t1op = ring(ph, "t1o", [128, 4, 128], F32, 2)
                        oap = ring(ph, "oa", [128, 4, 128], F32, 2)
                        ssp = ring(ph, "ssq", [128, 16], F32, 2)
                        oanp = ring(ph, "oan", [128, 4, 512], BF16, 2)
                        junk2 = T(sb(ph, "junk2", [128, 128], BF16))
                        pools = (xpool, xnpool, junk, stp)
                        tpool = Ring([(banks[7],)])
                        ppool = Ring([banks[7]])
                        spool = Ring(banks[0:3])
                        obanks = banks[3:7]
                        wsrc = wD_d.rearrange("(c p) n -> p c n", p=128)
                        for part in range(2):
                            S.add("pool", (lambda part: lambda e: e.dma_start(
                                out=w.t[:, :, part * 512:(part + 1) * 512],
                                in_=wsrc[:, :, 1536 + part * 512:1536 + (part + 1) * 512]))(part),
                                writes=[w.rb[part]], dma=True)
                        def prep_d(qb):
                            hT = hpool.next()
                            tab = tabp.next()
                            S.add("sp", (lambda tab, qb: lambda e: e.dma_start(
                                out=tab.t[:], in_=tabA_d[:, :, qb * 512:(qb + 1) * 512]))(tab, qb),
                                writes=[tab.b], dma=True)
                            h_block(pools, x_d[j, qb * 512:(qb + 1) * 512, :], 4, hT, j, 1, 0, tpool, ("dve",))
                            qAT = qpool.next()
                            for m in range(4):
                                ba = ppool.next()
                                proj_fm(ba, w, m * 128, 128, hT, 512)
                                t1 = t1p.next()
                                t2 = t2p.next()
                                S.add("dve", (lambda ba, t1, tab: lambda e: e.tensor_tensor(
                                    out=t1.t[:], in0=ba.t[:, :], in1=tab.t[:, 0, :], op=ALU.mult))(ba, t1, tab),
                                    reads=[ba.b, tab.b], writes=[t1.b])
                                S.rec_yield()
                                bb = ppool.next()
                                proj_fm(bb, w, 512 + m * 128, 128, hT, 512)
                                S.add("dve", (lambda bb, t2, tab: lambda e: e.tensor_tensor(
                                    out=t2.t[:], in0=bb.t[:, :], in1=tab.t[:, 1, :], op=ALU.mult))(bb, t2, tab),
                                    reads=[bb.b, tab.b], writes=[t2.b])
                                S.rec_yield()
                                for mp in range(2):
                                    S.add("pool", (lambda m, mp, t1, t2, qAT: lambda e: e.tensor_tensor(
                                        out=qAT.t[64 * mp:64 * mp + 64, 2 * m + mp, :], in0=t1.t[64 * mp:64 * mp + 64, :],
                                        in1=t2.t[64 * mp:64 * mp + 64, :], op=ALU.add))(m, mp, t1, t2, qAT),
                                        reads=[t1.b, t2.b], writes=[qAT.b])
                                S.rec_yield()
                            return qAT

                        XN_ENG[0] = "dve"
                        S.pad_on = True
                        qAT_next = prep_d(0)
                        tail_bg = []
                        for qb in range(4):
                            qAT = qAT_next
                            bg = None
                            if qb < 3:
                                S.rec_begin()
                                qAT_next = prep_d(qb + 1)
                                bg = S.rec_end()
                            oan = oanp.next()
                            groups = []
                            state = {}
                            for h in range(4):
                                for mp in range(2):
                                    def post(h=h, mp=mp, oan=oan):
                                        oraw = orawp.next()
                                        for t in range(4):
                                            S.add("dve", (lambda t, oraw: lambda e: e.tensor_copy(
                                                out=oraw.t[:, t, 0:129], in_=obanks[t].t[:, 0:129]))(t, oraw),
                                                reads=[obanks[t].b], writes=[oraw.b])
                                        r = rp.next()
                                        S.add("dve", lambda e: e.reciprocal(out=r.t[:, 0:4], in_=oraw.t[:, :, 128]),
                                              reads=[oraw.b], writes=[r.b])
                                        if mp == 0:
                                            state["oraw0"] = oraw
                                            state["r0"] = r
                                            return
                                        oraw0, r0 = state["oraw0"], state["r0"]
                                        S.add("dve", lambda e: e.tensor_scalar(out=r.t[:, 4:8], in0=r.t[:, 0:4],
                                                                               scalar1=nlam.t[:, 0:1], scalar2=None,
                                                                               op0=ALU.mult),
                                              reads=[r.b, nlam.b], writes=[r.b])
                                        t1o = t1op.next()
                                        oa = oap.next()
                                        ss = ssp.next()
                                        for t in range(4):
                                            S.add("dve", (lambda t: lambda e: e.tensor_scalar(
                                                out=t1o.t[:, t, :], in0=oraw0.t[:, t, 0:128], scalar1=r0.t[:, t:t + 1],
                                                scalar2=None, op0=ALU.mult))(t),
                                                reads=[oraw0.b, r0.b], writes=[t1o.b])
                                            S.add("dve", (lambda t: lambda e: e.scalar_tensor_tensor(
                                                out=oa.t[:, t, :], in0=oraw.t[:, t, 0:128], scalar=r.t[:, 4 + t:5 + t],
                                                in1=t1o.t[:, t, :], op0=ALU.mult, op1=ALU.add))(t),
                                                reads=[oraw.b, r.b, t1o.b], writes=[oa.b])
                                            S.add("dve", (lambda t: lambda e: e.scalar_tensor_tensor(
                                                out=junk2.t[:], in0=oa.t[:, t, :], scalar=1.0, in1=oa.t[:, t, :],
                                                op0=ALU.mult, op1=ALU.mult, accum_out=ss.t[:, t:t + 1]))(t),
                                                reads=[oa.b], writes=[ss.b])
                                        S.add("dve", lambda e: e.tensor_scalar(out=ss.t[:, 4:8], in0=ss.t[:, 0:4],
                                                                               scalar1=1.0 / 128, scalar2=EPS,
                                                                               op0=ALU.mult, op1=ALU.add),
                                              reads=[ss.b], writes=[ss.b])
                                        S.add("pool", lambda e: e.tensor_tensor(out=ss.t[:, 8:12], in0=ss.t[:, 4:8],
                                                                                in1=neghalf.t[:, 0:4], op=ALU.pow),
                                              reads=[ss.b, neghalf.b], writes=[ss.b])
                                        for t in range(4):
                                            S.add("dve", (lambda t: lambda e: e.tensor_scalar(
                                                out=oan.t[:, t, h * 128:(h + 1) * 128], in0=oa.t[:, t, :],
                                                scalar1=ss.t[:, 8 + t:9 + t], scalar2=None, op0=ALU.mult))(t),
                                                reads=[oa.b, ss.b], writes=[oan.b])

                                    base = 64 * mp
                                    groups.append(dict(
                                        qT=qAT.t[:, 2 * h + mp, :],
                                        kT=(lambda kc, h=h: kAT.t[:, h, kc * 128:(kc + 1) * 128]),
                                        v=(lambda kc, h=h: vA.t[:, kc, h * 129:(h + 1) * 129]),
                                        ncols=129, scale=0.125, qkreads=[qAT.b, kAT.b], vreads=[vA.b], post=post))
                            attention(groups, spool, obanks, ptp, bg=(tail_bg + (bg or [])), every=1)
                            S.rec_begin()
                            S.rec_pad(8)
                            for t in range(4):
                                tb = tpool.next()[0]
                                tbf = bf(tb)
                                for h in range(4):
                                    S.add("pe", (lambda t, h, tbf, oan: lambda e: e.transpose(
                                        tbf[:, h * 128:(h + 1) * 128], oan.t[:, t, h * 128:(h + 1) * 128], ident.t[:]))(t, h, tbf, oan),
                                        reads=[oan.b, ident.b], writes=[tb.b])
                                q0 = qb * 512 + t * 128
                                S.add("dve", (lambda tbf, q0: lambda e: e.tensor_scalar(
                                    out=catT.t[:, 0:4, q0:q0 + 128], in0=tbf[:, 0:512].rearrange("p (h q) -> p h q", h=4),
                                    scalar1=subln.t[:, 0:1], scalar2=None, op0=ALU.mult))(tbf, q0),
                                    reads=[tb.b, subln.b], writes=[catT.b])
                                S.rec_yield()
                            tail_bg = S.rec_end()
                        for g_ in tail_bg:
                            S.replay(g_)
                        XN_ENG[0] = "act"
                        S.pad_on = False
                        S.emit()

                with ExitStack() as pm:
                    KbT = T(sb(pm, "KbT", [128, 8, NKC * 128], BF16))
                    vB = T(sb(pm, "vB", [128, NKC, 8 * 65], BF16))

                    with ExitStack() as ph:
                        w = T(sb(ph, "wmk", [128, KC, 192], BF16))
                        wkvk = T(sb(ph, "wkvk", [128, 512], BF16))
                        wkvv = T(sb(ph, "wkvv", [128, 512], BF16))
                        hpool = ring(ph, "hT", [128, KC, 512], BF16, 2, KC)
                        xpool = ring(ph, "xt", [128, D], F32, 2)
                        xnpool = ring(ph, "xn", [128, D], BF16, 2)
                        junk = T(sb(ph, "junk", [128, D], BF16))
                        stp = ring(ph, "st", [128, 4], F32, 4)
                        tabp = ring(ph, "tab", [128, 2, 512], F32, 2)
                        t1p = ring(ph, "t1", [128, 512], F32, 2)
                        t2p = ring(ph, "t2", [128, 512], F32, 2)
                        sqp = ring(ph, "sq", [128, 512], F32, 2)
                        rsp = ring(ph, "rs", [128, 512], F32, 2)
                        ckvnp = ring(ph, "ckvn", [128, 512], BF16, 2)
                        krp = ring(ph, "kr", [128, 512], BF16, 2)
                        pools = (xpool, xnpool, junk, stp)
                        tpool = Ring([(banks[4], banks[5]), (banks[6], banks[7])])
                        ppool = Ring(banks[0:4])
                        S.add("pool", lambda e: e.dma_start(out=w.t[:], in_=wM_d.rearrange("(c p) n -> p c n", p=128)[:, :, 0:192]),
                              writes=[w.b], dma=True)
                        S.add("pool", lambda e: e.dma_start(out=wkvk.t[:], in_=wkvk_d), writes=[wkvk.b], dma=True)
                        S.add("pool", lambda e: e.dma_start(out=wkvv.t[:], in_=wkvv_d), writes=[wkvv.b], dma=True)
                        S.add("pool", lambda e: e.memset(vB.t[:], 1.0), writes=[vB.b])
                        def hb_m(blk):
                            lat = blk > 0
                            ntiles = 4 if lat else 2
                            src = x_d[j, (blk - 1) * 512:blk * 512, :] if lat else ctx_d[j]
                            hT = hpool.next()
                            tab = None
                            if lat:
                                tab = tabp.next()
                                S.add("sp", (lambda tab, blk: lambda e: e.dma_start(
                                    out=tab.t[:], in_=tabB_d[:, :, (blk - 1) * 512:blk * 512]))(tab, blk),
                                    writes=[tab.b], dma=True)
                            h_block(pools, src, ntiles, hT, j if lat else 2, 1, 0, tpool, ("dve", "act"))
                            return hT, tab

                        cur = hb_m(0)
                        for blk in range(5):
                            lat = blk > 0
                            ntiles = 4 if lat else 2
                            N = ntiles * 128
                            key0 = 0 if not lat else S_CTX + (blk - 1) * 512
                            hT, tab = cur
                            if blk < 4:
                                S.rec_begin()
                                cur = hb_m(blk + 1)
                                S.bg = S.rec_end()
                            bc = ppool.next()
                            proj_fm(bc, w, 0, 128, hT, N)
                            br = ppool.next()
                            proj_fm(br, w, 128, 32, hT, N)
                            kr = krp.next()
                            if lat:
                                brp = ppool.next()
                                proj_fm(brp, w, 160, 32, hT, N)
                                t1 = t1p.next()
                                t2 = t2p.next()
                                S.add("dve", (lambda br, t1, tab: lambda e: e.tensor_tensor(
                                    out=t1.t[0:32, :], in0=br.t[0:32, :], in1=tab.t[0:32, 0, :], op=ALU.mult))(br, t1, tab),
                                    reads=[br.b, tab.b], writes=[t1.b])
                                S.add("dve", (lambda brp, t2, tab: lambda e: e.tensor_tensor(
                                    out=t2.t[0:32, :], in0=brp.t[0:32, :], in1=tab.t[0:32, 1, :], op=ALU.mult))(brp, t2, tab),
                                    reads=[brp.b, tab.b], writes=[t2.b])
                                S.add("pool", (lambda t1, t2, kr: lambda e: e.tensor_tensor(
                                    out=kr.t[0:32, :], in0=t1.t[0:32, :], in1=t2.t[0:32, :], op=ALU.add))(t1, t2, kr),
                                    reads=[t1.b, t2.b], writes=[kr.b])
                            else:
                                S.add("act", (lambda br, kr, N: lambda e: e.activation(
                                    out=kr.t[0:32, 0:N], in_=br.t[0:32, 0:N], func=AF.Copy))(br, kr, N),
                                    reads=[br.b], writes=[kr.b])
                            S.add("act", (lambda kr, N, key0: lambda e: e.activation(
                                out=KbT.t[64:96, :, key0:key0 + N],
                                in_=kr.t[0:32, 0:N].unsqueeze(1).to_broadcast([32, 8, N]), func=AF.Copy))(kr, N, key0),
                                reads=[kr.b], writes=[KbT.b])
                            sq = sqp.next()
                            S.add("act", (lambda sq, bc, N: lambda e: e.activation(
                                out=sq.t[:, 0:N], in_=bc.t[:, 0:N], func=AF.Square))(sq, bc, N),
                                reads=[bc.b], writes=[sq.b])
                            bs = ppool.next()
                            S.add("pe", (lambda bs, sq, N: lambda e: e.matmul(
                                bs.t[:, 0:N], lhsT=ones_f.t[:], rhs=sq.t[:, 0:N], start=True, stop=True))(bs, sq, N),
                                reads=[ones_f.b, sq.b], writes=[bs.b])
                            rs = rsp.next()
                            S.add("dve", (lambda rs, bs, N: lambda e: e.tensor_scalar(
                                out=rs.t[:, 0:N], in0=bs.t[:, 0:N], scalar1=1.0 / 128, scalar2=EPS,
                                op0=ALU.mult, op1=ALU.add))(rs, bs, N), reads=[bs.b], writes=[rs.b])
                            S.add("act", (lambda rs, N: lambda e: e.activation(
                                out=rs.t[:, 0:N], in_=rs.t[:, 0:N], func=AF.Ln))(rs, N), reads=[rs.b], writes=[rs.b])
                            S.add("act", (lambda rs, N: lambda e: e.activation(
                                out=rs.t[:, 0:N], in_=rs.t[:, 0:N], func=AF.Exp, scale=-0.5))(rs, N),
                                reads=[rs.b], writes=[rs.b])
                            ckvn = ckvnp.next()
                            S.add("dve", (lambda ckvn, bc, rs, N: lambda e: e.scalar_tensor_tensor(
                                out=ckvn.t[:, 0:N], in0=bc.t[:, 0:N], scalar=smalls.t[:, C_KVN:C_KVN + 1],
                                in1=rs.t[:, 0:N], op0=ALU.mult, op1=ALU.mult))(ckvn, bc, rs, N),
                                reads=[bc.b, rs.b, smalls.b], writes=[ckvn.b])
                            for pj in range(4):
                                bk_ = ppool.next()
                                S.add("pe", (lambda pj, bk_, ckvn, N: lambda e: e.matmul(
                                    bk_.t[:, 0:N], lhsT=wkvk.t[:, pj * 128:(pj + 1) * 128], rhs=ckvn.t[:, 0:N],
                                    start=True, stop=True))(pj, bk_, ckvn, N),
                                    reads=[wkvk.b, ckvn.b], writes=[bk_.b])
                                S.add("act", (lambda pj, bk_, N, key0: lambda e: e.activation(
                                    out=KbT.t[0:64, 2 * pj, key0:key0 + N], in_=bk_.t[0:64, 0:N], func=AF.Copy))(pj, bk_, N, key0),
                                    reads=[bk_.b], writes=[KbT.b])
                                S.add("dve", (lambda pj, bk_, N, key0: lambda e: e.tensor_copy(
                                    out=KbT.t[0:64, 2 * pj + 1, key0:key0 + N], in_=bk_.t[64:128, 0:N]))(pj, bk_, N, key0),
                                    reads=[bk_.b], writes=[KbT.b])
                                S.tick(2)
                            for t in range(ntiles):
                                bv = ppool.next()
                                S.add("pe", (lambda t, bv, ckvn: lambda e: e.matmul(
                                    bv.t[:, :], lhsT=ckvn.t[:, t * 128:(t + 1) * 128], rhs=wkvv.t[:, :],
                                    start=True, stop=True))(t, bv, ckvn),
                                    reads=[ckvn.b, wkvv.b], writes=[bv.b])
                                kc = key0 // 128 + t
                                S.add("act", (lambda kc, bv: lambda e: e.activation(
                                    out=vB.t[:, kc, :].rearrange("p (h e) -> p h e", e=65)[:, :, 0:64],
                                    in_=bv.t[:, :].rearrange("p (h e) -> p h e", e=64), func=AF.Copy))(kc, bv),
                                    reads=[bv.b], writes=[vB.b])
                                S.tick(3)
                            S.flush_bg()
                        S.emit()

                    with ExitStack() as ph:
                        w = T(sb(ph, "wmq", [128, KC, 256], BF16))
                        wqup = T(sb(ph, "wqup", [128, 2, 768], BF16))
                        wqupp = T(sb(ph, "wqupp", [128, 2, 256], BF16))
                        hpool = ring(ph, "hT", [128, KC, 512], BF16, 1, KC)
                        xpool = ring(ph, "xt", [128, D], F32, 2)
                        xnpool = ring(ph, "xn", [128, D], BF16, 2)
                        junk = T(sb(ph, "junk", [128, D], BF16))
                        stp = ring(ph, "st", [128, 4], F32, 4)
                        tabp = ring(ph, "tab", [128, 2, 512], F32, 2)
                        t1p = ring(ph, "t1", [128, 512], F32, 2)
                        t2p = ring(ph, "t2", [128, 512], F32, 2)
                        sqp = ring(ph, "sq", [128, 2, 512], F32, 1)
                        cqrawp = ring(ph, "cqraw", [128, 2, 512], F32, 1)
                        rsp = ring(ph, "rs", [128, 512], F32, 1)
                        cqnp = ring(ph, "cqn", [128, 2, 512], BF16, 1)
                        qpool = ring(ph, "qbT", [128, 8, 512], BF16, 2)
                        ptp = ring(ph, "pt", [128, 512], BF16, 4)
                        orawp = ring(ph, "oraw", [128, 4, 68], F32, 3)
                        rp = ring(ph, "rr", [128, 4], F32, 4)
                        obnp = ring(ph, "obn", [128, 4, 512], BF16, 2)
                        pools = (xpool, xnpool, junk, stp)
                        tpool = Ring([(banks[7],)])
                        ppool = Ring([banks[7]])
                        spool = Ring(banks[0:3])
                        obanks = banks[3:7]
                        S.add("pool", lambda e: e.dma_start(out=w.t[:], in_=wM_d.rearrange("(c p) n -> p c n", p=128)[:, :, 192:448]),
                              writes=[w.b], dma=True)
                        S.add("pool", lambda e: e.dma_start(out=wqup.t[:], in_=wqup_d.rearrange("(c p) n -> p c n", p=128)),
                              writes=[wqup.b], dma=True)
                        S.add("pool", lambda e: e.dma_start(out=wqupp.t[:], in_=wqupp_d.rearrange("(c p) n -> p c n", p=128)),
                              writes=[wqupp.b], dma=True)
                        def prep_m(qb):
                            hT = hpool.next()
                            tab = tabp.next()
                            S.add("sp", (lambda tab, qb: lambda e: e.dma_start(
                                out=tab.t[:], in_=tabB_d[:, :, qb * 512:(qb + 1) * 512]))(tab, qb),
                                writes=[tab.b], dma=True)
                            h_block(pools, x_d[j, qb * 512:(qb + 1) * 512, :], 4, hT, j, 1, 0, tpool, ("dve",))
                            cqraw = cqrawp.next()
                            sq = sqp.next()
                            for mc in range(2):
                                bq = ppool.next()
                                proj_fm(bq, w, mc * 128, 128, hT, 512)
                                S.add("dve", (lambda mc, cqraw, bq: lambda e: e.tensor_copy(
                                    out=cqraw.t[:, mc, :], in_=bq.t[:, :]))(mc, cqraw, bq),
                                    reads=[bq.b], writes=[cqraw.b])
                                S.rec_yield()
                                S.add("dve", (lambda mc, sq, cqraw: lambda e: e.tensor_tensor(
                                    out=sq.t[:, mc, :], in0=cqraw.t[:, mc, :], in1=cqraw.t[:, mc, :], op=ALU.mult))(mc, sq, cqraw),
                                    reads=[cqraw.b], writes=[sq.b])
                                S.rec_yield()
                            bs = ppool.next()
                            for mc in range(2):
                                S.add("pe", (lambda mc, bs, sq: lambda e: e.matmul(
                                    bs.t[:, :], lhsT=ones_f.t[:], rhs=sq.t[:, mc, :], start=(mc == 0), stop=(mc == 1)))(mc, bs, sq),
                                    reads=[ones_f.b, sq.b], writes=[bs.b])
                            S.rec_yield()
                            rs = rsp.next()
                            S.add("dve", (lambda rs, bs: lambda e: e.tensor_scalar(
                                out=rs.t[:], in0=bs.t[:, :], scalar1=1.0 / 256, scalar2=EPS, op0=ALU.mult, op1=ALU.add))(rs, bs),
                                reads=[bs.b], writes=[rs.b])
                            S.rec_pad(3)
                            S.add("act", (lambda rs: lambda e: e.activation(
                                out=rs.t[:], in_=rs.t[:], func=AF.Ln))(rs), reads=[rs.b], writes=[rs.b])
                            S.add("act", (lambda rs: lambda e: e.activation(
                                out=rs.t[:], in_=rs.t[:], func=AF.Exp, scale=-0.5))(rs), reads=[rs.b], writes=[rs.b])
                            S.rec_yield()
                            cqn = cqnp.next()
                            for mc in range(2):
                                S.add("dve", (lambda mc, cqn, cqraw, rs: lambda e: e.scalar_tensor_tensor(
                                    out=cqn.t[:, mc, :], in0=cqraw.t[:, mc, :], scalar=smalls.t[:, C_QN + mc:C_QN + mc + 1],
                                    in1=rs.t[:], op0=ALU.mult, op1=ALU.mult))(mc, cqn, cqraw, rs),
                                    reads=[cqraw.b, rs.b, smalls.b], writes=[cqn.b])
                            S.rec_yield()
                            qbT = qpool.next()
                            for h in range(8):
                                bp = ppool.next()
                                for kc2 in range(2):
                                    S.add("pe", (lambda h, kc2, bp, cqn: lambda e: e.matmul(
                                        bp.t[0:96, :], lhsT=wqup.t[:, kc2, h * 96:(h + 1) * 96], rhs=cqn.t[:, kc2, :],
                                        start=(kc2 == 0), stop=(kc2 == 1)))(h, kc2, bp, cqn),
                                        reads=[wqup.b, cqn.b], writes=[bp.b])
                                S.rec_yield()
                                S.add("dve", (lambda h, bp, qbT: lambda e: e.tensor_copy(
                                    out=qbT.t[0:64, h, :], in_=bp.t[0:64, :]))(h, bp, qbT),
                                    reads=[bp.b], writes=[qbT.b])
                                t1 = t1p.next()
                                t2 = t2p.next()
                                S.add("dve", (lambda bp, t1, tab: lambda e: e.tensor_tensor(
                                    out=t1.t[64:96, :], in0=bp.t[64:96, :], in1=tab.t[64:96, 0, :], op=ALU.mult))(bp, t1, tab),
                                    reads=[bp.b, tab.b], writes=[t1.b])
                                S.rec_yield()
                                bpp = ppool.next()
                                for kc2 in range(2):
                                    S.add("pe", (lambda h, kc2, bpp, cqn: lambda e: e.matmul(
                                        bpp.t[0:32, :], lhsT=wqupp.t[:, kc2, h * 32:(h + 1) * 32], rhs=cqn.t[:, kc2, :],
                                        start=(kc2 == 0), stop=(kc2 == 1)))(h, kc2, bpp, cqn),
                                        reads=[wqupp.b, cqn.b], writes=[bpp.b])
                                S.rec_yield()
                                S.add("dve", (lambda bpp, t2, tab: lambda e: e.tensor_tensor(
                                    out=t2.t[64:96, :], in0=bpp.t[0:32, :], in1=tab.t[64:96, 1, :], op=ALU.mult))(bpp, t2, tab),
                                    reads=[bpp.b, tab.b], writes=[t2.b])
                                S.add("pool", (lambda h, t1, t2, qbT: lambda e: e.tensor_tensor(
                                    out=qbT.t[64:96, h, :], in0=t1.t[64:96, :], in1=t2.t[64:96, :], op=ALU.add))(h, t1, t2, qbT),
                                    reads=[t1.b, t2.b], writes=[qbT.b])
                                S.rec_yield()
                            return qbT

                        XN_ENG[0] = "dve"
                        S.pad_on = True
                        qbT_next = prep_m(0)
                        tail_bg = []
                        for qb in range(4):
                            qbT = qbT_next
                            bg = None
                            if qb < 3:
                                S.rec_begin()
                                qbT_next = prep_m(qb + 1)
                                bg = S.rec_end()
                            obn = obnp.next()
                            groups = []
                            for h in range(8):
                                def post(h=h, obn=obn):
                                    oraw = orawp.next()
                                    for t in range(4):
                                        S.add("dve", (lambda t, oraw: lambda e: e.tensor_copy(
                                            out=oraw.t[:, t, 0:65], in_=obanks[t].t[:, 0:65]))(t, oraw),
                                            reads=[obanks[t].b], writes=[oraw.b])
                                    r = rp.next()
                                    S.add("dve", lambda e: e.reciprocal(out=r.t[:, 0:4], in_=oraw.t[:, :, 64]),
                                          reads=[oraw.b], writes=[r.b])
                                    for t in range(4):
                                        S.add("dve", (lambda t: lambda e: e.tensor_scalar(
                                            out=obn.t[:, t, h * 64:(h + 1) * 64], in0=oraw.t[:, t, 0:64],
                                            scalar1=r.t[:, t:t + 1], scalar2=None, op0=ALU.mult))(t),
                                            reads=[oraw.b, r.b], writes=[obn.b])

                                groups.append(dict(
                                    qT=qbT.t[0:96, h, :],
                                    kT=(lambda kc, h=h: KbT.t[0:96, h, kc * 128:(kc + 1) * 128]),
                                    v=(lambda kc, h=h: vB.t[:, kc, h * 65:(h + 1) * 65]),
                                    ncols=65, scale=96 ** -0.5, qkreads=[qbT.b, KbT.b], vreads=[vB.b], post=post))
                            attention(groups, spool, obanks, ptp, bg=(tail_bg + (bg or [])), every=1)
                            S.rec_begin()
                            S.rec_pad(8)
                            for t in range(4):
                                tb = tpool.next()[0]
                                tbf = bf(tb)
                                for c in range(4):
                                    S.add("pe", (lambda t, c, tbf, obn: lambda e: e.transpose(
                                        tbf[:, c * 128:(c + 1) * 128], obn.t[:, t, c * 128:(c + 1) * 128], ident.t[:]))(t, c, tbf, obn),
                                        reads=[obn.b, ident.b], writes=[tb.b])
                                q0 = qb * 512 + t * 128
                                S.add("dve", (lambda tbf, q0: lambda e: e.tensor_copy(
                                    out=catT.t[:, 4:8, q0:q0 + 128], in_=tbf[:, 0:512].rearrange("p (h q) -> p h q", h=4)))(tbf, q0),
                                    reads=[tb.b], writes=[catT.b])
                                S.rec_yield()
                            tail_bg = S.rec_end()
                        for g_ in tail_bg:
                            S.replay(g_)
                        XN_ENG[0] = "act"
                        S.pad_on = False
                        S.emit()

                with ExitStack() as pf:
                    acc = [T(None) for _ in range(16)]
                    acc_t = sb(pf, "acc", [128, 16, D], F32)
                    h2T = [T(None, KC) for _ in range(4)]
                    h2T_t = sb(pf, "h2T", [128, KC, S_LAT], BF16)

                    with ExitStack() as ph:
                        wout = T(sb(ph, "wout", [128, KC, D], BF16))
                        xpool = ring(ph, "xt", [128, D], F32, 2)
                        xnpool = ring(ph, "xn", [128, D], BF16, 2)
                        junk = T(sb(ph, "junk", [128, D], BF16))
                        stp = ring(ph, "st", [128, 4], F32, 4)
                        tmpp = ring(ph, "tmpo", [128, 512], F32, 2)
                        tpool = Ring([(banks[4], banks[5]), (banks[6], banks[7])])
                        ppool = Ring(banks[0:4])
                        S.add("pool", lambda e: e.dma_start(out=wout.t[:], in_=wout_d.rearrange("(c p) n -> p c n", p=128)),
                              writes=[wout.b], dma=True)

                        class HV:
                            pass

                        prev = None
                        for tt in range(16):
                            xt = xpool.next()
                            S.add("sp", (lambda xt, tt: lambda e: e.dma_start(
                                out=xt.t[:], in_=x_d[j, tt * 128:(tt + 1) * 128, :]))(xt, tt), writes=[xt.b], dma=True)
                            a = acc[tt]
                            for half in range(2):
                                bo = ppool.next()
                                for c in range(KC):
                                    S.add("pe", (lambda tt, half, c, bo: lambda e: e.matmul(
                                        bo.t[:, :], lhsT=catT.t[:, c, tt * 128:(tt + 1) * 128],
                                        rhs=wout.t[:, c, half * 512:(half + 1) * 512],
                                        start=(c == 0), stop=(c == KC - 1)))(tt, half, c, bo),
                                        reads=[catT.b, wout.b], writes=[bo.b])
                                tmp = tmpp.next()
                                S.add("dve", (lambda half, bo, tmp: lambda e: e.tensor_tensor(
                                    out=tmp.t[:], in0=bo.t[:, :], in1=G.t[:, j, 0, half * 512:(half + 1) * 512],
                                    op=ALU.mult))(half, bo, tmp), reads=[bo.b, G.b], writes=[tmp.b])
                                S.add("pool", (lambda tt, half, tmp, xt: lambda e: e.tensor_tensor(
                                    out=acc_t[:, tt, half * 512:(half + 1) * 512], in0=tmp.t[:],
                                    in1=xt.t[:, half * 512:(half + 1) * 512], op=ALU.add))(tt, half, tmp, xt),
                                    reads=[tmp.b, xt.b], writes=[a.b])
                            av = HV()
                            av.t = acc_t[:, tt, :]
                            av.b = a.b
                            xn_cur = norm_a((xpool, xnpool, junk, stp), av)
                            if PIPE_O and prev is not None:
                                trans_b(prev[0], prev[1] % 4, prev[2], j, 3, 2, tpool, ("dve", "act"))
                            hv = HV()
                            blk = tt // 4
                            hv.t = h2T_t[:, :, blk * 512:(blk + 1) * 512]
                            hv.rb = h2T[blk].rb
                            prev = (xn_cur, tt, hv)
                            if not PIPE_O:
                                trans_b(prev[0], prev[1] % 4, prev[2], j, 3, 2, tpool, ("dve", "act"))
                        if PIPE_O:
                            trans_b(prev[0], prev[1] % 4, prev[2], j, 3, 2, tpool, ("dve", "act"))
                        S.emit()

                    with ExitStack() as ph:
                        wg_views = [T(catT.t[:, 4 * i:4 * i + 4, :].rearrange("p a (b n) -> p (a b) n", n=1024), 2)
                                    for i in range(2)]
                        wgp = Ring(wg_views)
                        wop = ring(ph, "wo", [128, 4, D], BF16, 2)
                        actp = ring(ph, "actT", [128, 4, 512], BF16, 2)
                        sgp = ring(ph, "sg", [128, 512], F32, 3)
                        otp = ring(ph, "ot", [128, D], F32, 2)
                        junkf = T(sb(ph, "junkf", [128, D], BF16))
                        stp = ring(ph, "st", [128, 4], F32, 4)
                        gupool = Ring(banks[0:4])
                        fopool = Ring(banks[4:8])
                        wfi_src = wfi_d.rearrange("(c p) n -> p c n", p=128)
                        wfo_src = wfo_d.rearrange("(c p) n -> p c n", p=128)
                        ngroups = (NHC + 3) // 4
                        for g in range(ngroups):
                            c0 = g * 4
                            ng = min(4, NHC - c0)
                            wg = wgp.next()
                            wo = wop.next()
                            S.add("pool", (lambda wg, c0, ng: lambda e: e.dma_start(
                                out=wg.t[:, :, 0:ng * 128], in_=wfi_src[:, :, c0 * 128:(c0 + ng) * 128]))(wg, c0, ng),
                                writes=[wg.rb[0]], dma=True)
                            S.add("pool", (lambda wg, c0, ng: lambda e: e.dma_start(
                                out=wg.t[:, :, 512:512 + ng * 128],
                                in_=wfi_src[:, :, FFN_H + c0 * 128:FFN_H + (c0 + ng) * 128]))(wg, c0, ng),
                                writes=[wg.rb[1]], dma=True)
                            S.add("pool", (lambda wo, c0, ng: lambda e: e.dma_start(
                                out=wo.t[:, 0:ng, :], in_=wfo_src[:, c0:c0 + ng, :]))(wo, c0, ng),
                                writes=[wo.b], dma=True)
                            for ci in range(ng):
                                S.add("pool", (lambda wo, ci: lambda e: e.tensor_tensor(
                                    out=wo.t[:, ci, :], in0=wo.t[:, ci, :], in1=G.t[:, j, 1, :], op=ALU.mult))(wo, ci),
                                    reads=[wo.b, G.b], writes=[wo.b])
                            for tb4 in range(4):
                                actT = actp.next()
                                hb = h2T[tb4]
                                for ci in range(ng):
                                    bg = gupool.next()
                                    bu = gupool.next()
                                    for c in range(KC):
                                        S.add("pe", (lambda c, ci, bg, wg, tb4: lambda e: e.matmul(
                                            bg.t[:, :], lhsT=wg.t[:, c, ci * 128:(ci + 1) * 128],
                                            rhs=h2T_t[:, c, tb4 * 512:(tb4 + 1) * 512],
                                            start=(c == 0), stop=(c == KC - 1)))(c, ci, bg, wg, tb4),
                                            reads=[wg.rb[0], hb.rb[c]], writes=[bg.b])
                                    for c in range(KC):
                                        S.add("pe", (lambda c, ci, bu, wg, tb4: lambda e: e.matmul(
                                            bu.t[:, :], lhsT=wg.t[:, c, 512 + ci * 128:512 + (ci + 1) * 128],
                                            rhs=h2T_t[:, c, tb4 * 512:(tb4 + 1) * 512],
                                            start=(c == 0), stop=(c == KC - 1)))(c, ci, bu, wg, tb4),
                                            reads=[wg.rb[1], hb.rb[c]], writes=[bu.b])
                                    sg = sgp.next()
                                    S.add("act", (lambda sg, bg: lambda e: e.activation(
                                        out=sg.t[:], in_=bg.t[:, :], func=AF.Silu))(sg, bg),
                                        reads=[bg.b], writes=[sg.b])
                                    S.add("dve", (lambda ci, sg, bu, actT: lambda e: e.tensor_tensor(
                                        out=actT.t[:, ci, :], in0=bu.t[:, :], in1=sg.t[:], op=ALU.mult))(ci, sg, bu, actT),
                                        reads=[bu.b, sg.b], writes=[actT.b])
                                for t in range(4):
                                    tt = tb4 * 4 + t
                                    for half in range(2):
                                        bo = fopool.next()
                                        for ci in range(ng):
                                            S.add("pe", (lambda t, half, ci, bo, actT, wo, ng: lambda e: e.matmul(
                                                bo.t[:, :], lhsT=actT.t[:, ci, t * 128:(t + 1) * 128],
                                                rhs=wo.t[:, ci, half * 512:(half + 1) * 512],
                                                start=(ci == 0), stop=(ci == ng - 1)))(t, half, ci, bo, actT, wo, ng),
                                                reads=[actT.b, wo.b], writes=[bo.b])
                                        eng = "dve" if half == 0 else "dve"
                                        S.add(eng, (lambda tt, half, bo: lambda e: e.tensor_tensor(
                                            out=acc_t[:, tt, half * 512:(half + 1) * 512], in0=bo.t[:, :],
                                            in1=acc_t[:, tt, half * 512:(half + 1) * 512], op=ALU.add))(tt, half, bo),
                                            reads=[bo.b, acc[tt].b], writes=[acc[tt].b])
                        for tt in range(16):
                            s = stp.next()
                            ot = otp.next()
                            S.add("dve", (lambda tt, s: lambda e: e.scalar_tensor_tensor(
                                out=junkf.t[:], in0=acc_t[:, tt, :], scalar=1.0, in1=acc_t[:, tt, :],
                                op0=ALU.mult, op1=ALU.mult, accum_out=s.t[:, 0:1]))(tt, s),
                                reads=[acc[tt].b], writes=[s.b])
                            S.add("dve", (lambda s: lambda e: e.tensor_scalar(
                                out=s.t[:, 1:2], in0=s.t[:, 0:1], scalar1=1.0 / D, scalar2=EPS,
                                op0=ALU.mult, op1=ALU.add))(s), reads=[s.b], writes=[s.b])
                            S.add("pool", (lambda s: lambda e: e.tensor_tensor(
                                out=s.t[:, 2:3], in0=s.t[:, 1:2], in1=neghalf.t[:, 0:1], op=ALU.pow))(s),
                                reads=[s.b, neghalf.b], writes=[s.b])
                            S.add("dve", (lambda tt, s, ot: lambda e: e.scalar_tensor_tensor(
                                out=ot.t[:], in0=acc_t[:, tt, :], scalar=s.t[:, 2:3], in1=fnorm_bc.t[:],
                                op0=ALU.mult, op1=ALU.mult))(tt, s, ot),
                                reads=[acc[tt].b, s.b, fnorm_bc.b], writes=[ot.b])
                            S.add("sp", (lambda tt, ot: lambda e: e.dma_start(
                                out=y_d[j, tt * 128:(tt + 1) * 128, :], in_=ot.t[:]))(tt, ot),
                                reads=[ot.b], dma=True)
                        S.emit()
    return nc


def _rope_tables():
    t = np.arange(S_LAT)
    row = (t // GRID_W).astype(np.float64)
    col = (t % GRID_W).astype(np.float64)
    tabA = np.zeros((128, 2, S_LAT), np.float32)
    for p in range(128):
        d = p % 64
        pos = row if d < 32 else col
        inv = THETA ** (-(d % 16) / 16.0)
        ang = pos * inv
        sign = -1.0 if (d % 32) < 16 else 1.0
        tabA[p, 0] = np.cos(ang)
        tabA[p, 1] = sign * np.sin(ang)
    tabB = np.zeros((128, 2, S_LAT), np.float32)
    for p in list(range(0, 32)) + list(range(64, 96)):
        d = p % 32
        pos = row if d < 16 else col
        inv = THETA ** (-(d % 8) / 8.0)
        ang = pos * inv
        sign = -1.0 if (d % 16) < 8 else 1.0
        tabB[p, 0] = np.cos(ang)
        tabB[p, 1] = sign * np.sin(ang)
    return tabA, tabB


_NC_CACHE = {}


def kernel(x, c, ctx, c_ctx, w_ada, b_ada, w_in, q_a_norm, kv_a_norm, w_q_up, w_kv_up,
           diff_lambda, diff_subln, w_out, w_ffn_in, w_ffn_out, final_norm):
    f32 = np.float32
    x = np.asarray(x, f32)
    c = np.asarray(c, f32)
    ctx = np.asarray(ctx, f32)
    c_ctx = np.asarray(c_ctx, f32)
    w_ada0 = np.ascontiguousarray(np.asarray(w_ada, f32)[0])
    b_ada0 = np.asarray(b_ada, f32)[0]
    w_in0 = np.asarray(w_in, f32)[0]
    w_q_up0 = np.asarray(w_q_up, f32)[0]
    w_kv_up0 = np.asarray(w_kv_up, f32)[0]

    perm64 = np.array([i + 16 if (i % 32) < 16 else i - 16 for i in range(64)])
    perm32 = np.array([i + 8 if (i % 16) < 8 else i - 8 for i in range(32)])
    permA = np.concatenate([64 * b + perm64 for b in range(8)])
    qA, kA, vA = w_in0[:, 0:512], w_in0[:, 512:1024], w_in0[:, 1024:1536]
    cq, ckv, kr = w_in0[:, 1536:1792], w_in0[:, 1792:1920], w_in0[:, 1920:1952]
    w_in_d = np.ascontiguousarray(np.concatenate([kA, kA[:, permA], vA, qA, qA[:, permA]], axis=1))
    w_in_m = np.ascontiguousarray(np.concatenate([ckv, kr, kr[:, perm32], cq], axis=1))
    rope_cols = np.concatenate([96 * h + 64 + perm32 for h in range(8)])
    w_qupp = np.ascontiguousarray(w_q_up0[:, rope_cols])
    kcols = np.concatenate([128 * h + np.arange(64) for h in range(8)])
    vcols = np.concatenate([128 * h + 64 + np.arange(64) for h in range(8)])
    w_kvk = np.ascontiguousarray(w_kv_up0[:, kcols])
    w_kvv = np.ascontiguousarray(w_kv_up0[:, vcols])
    tabA, tabB = _rope_tables()

    bigs = np.empty((128, 3072), f32)
    bigs[:, 0:1024] = b_ada0[None, 2048:3072]
    bigs[:, 1024:2048] = b_ada0[None, 5120:6144]
    bigs[:, 2048:3072] = np.asarray(final_norm, f32)[None, :]

    def fm(v):
        return np.asarray(v, f32).reshape(-1, 128).T

    shared = {
        "bigs": bigs, "w_ada": w_ada0, "w_in_d": w_in_d, "w_in_m": w_in_m,
        "w_qup": np.ascontiguousarray(w_q_up0), "w_qupp": w_qupp, "w_kvk": w_kvk, "w_kvv": w_kvv,
        "w_out": np.ascontiguousarray(np.asarray(w_out, f32)[0]),
        "w_ffn_in": np.ascontiguousarray(np.asarray(w_ffn_in, f32)[0]),
        "w_ffn_out": np.ascontiguousarray(np.asarray(w_ffn_out, f32)[0]),
        "tab_a": tabA, "tab_b": tabB,
    }
    in_maps = []
    for i in range(N_CORES):
        sm = np.zeros((128, 320), f32)
        cvecs = [c[2 * i], c[2 * i + 1], c_ctx]
        for jv, v in enumerate(cvecs):
            sm[:, 0 + jv:24:3] = fm(v)
        for ki, kind in enumerate((0, 1, 3, 4)):
            sm[:, 24 + ki * 8:24 + (ki + 1) * 8] = fm(b_ada0[kind * 1024:(kind + 1) * 1024])
        sm[:, 56:312] = np.asarray(diff_lambda, f32)[0].reshape(1, 256)
        sm[:, 312] = np.asarray(diff_subln, f32)[0]
        sm[:, 313:315] = fm(np.asarray(q_a_norm, f32)[0])
        sm[:, 315] = np.asarray(kv_a_norm, f32)[0]
        m = dict(shared)
        m["x"] = np.ascontiguousarray(x[2 * i:2 * i + 2])
        m["ctx"] = np.ascontiguousarray(ctx[2 * i:2 * i + 2])
        m["smalls"] = sm
        in_maps.append(m)

    if "nc" not in _NC_CACHE:
        _NC_CACHE["nc"] = build_program()
    nc = _NC_CACHE["nc"]
    res = run_bass_kernel_spmd(nc, in_maps, core_ids=list(range(N_CORES)))
    out = np.concatenate([np.asarray(r["y"], f32) for r in res.results], axis=0)
    return out
```

```python
import math
from contextlib import ExitStack

import numpy as np
import concourse.bass as bass
import concourse.mybir as mybir
from concourse.bass_utils import run_bass_kernel_spmd

F32 = mybir.dt.float32
BF16 = mybir.dt.bfloat16
AF = mybir.ActivationFunctionType
ALU = mybir.AluOpType

N_CORES = 8
NB = 2
D = 1024
KC = 8
S_LAT = 2048
S_CTX = 256
NKC = 18
FFN_H = 2816
NHC = 22
EPS = 1e-6
LAM_INIT = 0.8 - 0.6 * math.exp(-0.3 * 0)
THETA = 10000.0
GRID_W = 64
import os
PIPE_H = int(os.environ.get('PIPE_H', '1'))
SAME_SYNC = tuple(x for x in os.environ.get('SAME_SYNC', 'dve,act,pool').split(',') if x)
PIPE_O = int(os.environ.get('PIPE_O', '1'))

ENGS = ("pe", "act", "dve", "pool", "sp")
CMP_ENGS = ("pe", "act", "dve", "pool")
DMA_ENGS = ("sp", "act", "pool")
SEM_CHUNK = 6000
N_CMP_SEMS = 6
N_DMA_SEMS = 10


class Buf:
    __slots__ = ("w", "r", "excl")

    def __init__(self):
        self.w = None
        self.r = {}
        self.excl = False


class Op:
    __slots__ = ("eng", "fn", "deps", "is_dma", "need_inc", "sem", "val", "phase")

    def __init__(self, eng, fn, is_dma, phase):
        self.eng = eng
        self.fn = fn
        self.is_dma = is_dma
        self.deps = []
        self.need_inc = False
        self.sem = None
        self.val = None
        self.phase = phase


class Sched:
    def __init__(self, nc, es):
        self.nc = nc
        self.sems = {}
        for e in CMP_ENGS:
            for i in range(N_CMP_SEMS):
                self.sems[("cmp", e, i)] = es.enter_context(nc.semaphore("c_%s_%d" % (e, i)))
        for e in DMA_ENGS:
            for i in range(N_DMA_SEMS):
                self.sems[("dma", e, i)] = es.enter_context(nc.semaphore("d_%s_%d" % (e, i)))
        self.cnt = {e: 0 for e in ENGS}
        self.dcnt = {e: 0 for e in ENGS}
        self.seen = {e: {} for e in ENGS}
        self.phase = 0
        self.rec = None
        self.pad_on = False
        self.bg = []
        self.ops = {e: [] for e in ENGS}
        self.all = []

    def rec_begin(self):
        self.rec = [[]]

    def rec_yield(self):
        if self.rec is not None and self.rec[-1]:
            self.rec.append([])

    def rec_pad(self, n=1):
        if self.rec is not None:
            if self.rec[-1]:
                self.rec.append([])
            if not self.pad_on:
                return
            for _ in range(n):
                self.rec.append([None])
                self.rec.append([])

    def rec_end(self):
        r = [g for g in self.rec if g]
        self.rec = None
        return r

    def replay(self, group):
        for args in group:
            if args is not None:
                self.add(*args)

    def tick(self, n=1):
        if self.rec is not None:
            return
        while n > 0 and self.bg:
            self.replay(self.bg.pop(0))
            n -= 1

    def flush_bg(self):
        self.tick(1 << 30)

    def add(self, eng, fn, reads=(), writes=(), dma=False):
        if self.rec is not None:
            self.rec[-1].append((eng, fn, tuple(reads), tuple(writes), dma))
            return None
        op = Op(eng, fn, dma, self.phase)
        seen = set()
        deps = op.deps
        ph = self.phase
        for b in reads:
            d = b.w
            if d is not None and d.phase == ph and id(d) not in seen:
                seen.add(id(d))
                deps.append(d)
            if b.excl:
                for d in b.r.values():
                    if d.eng != eng and d.phase == ph and id(d) not in seen:
                        seen.add(id(d))
                        deps.append(d)
        for b in writes:
            d = b.w
            if d is not None and d.phase == ph and id(d) not in seen:
                seen.add(id(d))
                deps.append(d)
            for d in b.r.values():
                if d.phase == ph and id(d) not in seen:
                    seen.add(id(d))
                    deps.append(d)
        for b in writes:
            b.w = op
            b.r = {}
        for b in reads:
            if b.w is not op:
                b.r[id(op) if dma else eng] = op
        self.all.append(op)
        self.ops[eng].append(op)
        return op

    @staticmethod
    def _needs_wait(op, d):
        if d.is_dma:
            return True
        if d.eng != op.eng:
            return True
        if op.eng == "pe":
            return False
        if op.is_dma:
            return True
        return op.eng in SAME_SYNC

    def emit(self):
        nc = self.nc
        for op in self.all:
            for d in op.deps:
                if self._needs_wait(op, d):
                    d.need_inc = True
        for e in ENGS:
            last = None
            for op in self.ops[e]:
                if op.is_dma:
                    op.need_inc = True
                else:
                    last = op
            if last is not None:
                last.need_inc = True
        for op in self.all:
            if not op.need_inc:
                continue
            e = op.eng
            if op.is_dma:
                k = self.dcnt[e]
                op.sem = ("dma", e, k % N_DMA_SEMS)
                op.val = 16 * (k // N_DMA_SEMS + 1)
                self.dcnt[e] = k + 1
            else:
                k = self.cnt[e]
                assert k < SEM_CHUNK * N_CMP_SEMS, "too many semaphore increments on " + e
                op.sem = ("cmp", e, k // SEM_CHUNK)
                op.val = k % SEM_CHUNK + 1
                self.cnt[e] = k + 1
        targets = []
        for e in CMP_ENGS:
            k = self.cnt[e]
            if k > 0:
                targets.append((("cmp", e, (k - 1) // SEM_CHUNK), (k - 1) % SEM_CHUNK + 1))
        for e in DMA_ENGS:
            k = self.dcnt[e]
            for i in range(N_DMA_SEMS):
                n = (k - i + N_DMA_SEMS - 1) // N_DMA_SEMS if k > i else 0
                if n > 0:
                    targets.append((("dma", e, i), 16 * n))
        sems = self.sems
        sched = self

        def run(e, eng):
            seen = sched.seen[e]
            for op in sched.ops[e]:
                waits = {}
                for d in op.deps:
                    if not sched._needs_wait(op, d):
                        continue
                    if waits.get(d.sem, 0) < d.val:
                        waits[d.sem] = d.val
                for k, v in waits.items():
                    if seen.get(k, 0) >= v:
                        continue
                    seen[k] = v
                    eng.wait_ge(sems[k], v)
                ins = op.fn(eng)
                if op.need_inc:
                    ins.then_inc(sems[op.sem], 16 if op.is_dma else 1)
            for k, v in targets:
                if seen.get(k, 0) >= v:
                    continue
                seen[k] = v
                eng.wait_ge(sems[k], v)

        with nc.Block() as block:
            @block.tensor
            def _(eng):
                run("pe", eng)

            @block.scalar
            def _(eng):
                run("act", eng)

            @block.vector
            def _(eng):
                run("dve", eng)

            @block.gpsimd
            def _(eng):
                run("pool", eng)

            @block.sync
            def _(eng):
                run("sp", eng)

        self.phase += 1
        self.ops = {e: [] for e in ENGS}
        self.all = []


class T:
    __slots__ = ("t", "b", "rb")

    def __init__(self, t, nparts=1):
        self.t = t
        self.rb = [Buf() for _ in range(nparts)]
        self.b = self.rb[0]


class Ring:
    def __init__(self, items):
        self.items = items
        self.i = 0

    def next(self):
        it = self.items[self.i % len(self.items)]
        self.i += 1
        return it


def build_program():
    nc = bass.Bass("TRN2", target_bir_lowering=False)

    def din(name, shape):
        return nc.dram_tensor(name, list(shape), F32, kind="ExternalInput").ap()

    x_d = din("x", [NB, S_LAT, D])
    ctx_d = din("ctx", [NB, S_CTX, D])
    smalls_d = din("smalls", [128, 320])
    bigs_d = din("bigs", [128, 3072])
    wada_d = din("w_ada", [D, 6 * D])
    wD_d = din("w_in_d", [D, 2560])
    wM_d = din("w_in_m", [D, 448])
    wqup_d = din("w_qup", [256, 768])
    wqupp_d = din("w_qupp", [256, 256])
    wkvk_d = din("w_kvk", [128, 512])
    wkvv_d = din("w_kvv", [128, 512])
    wout_d = din("w_out", [D, D])
    wfi_d = din("w_ffn_in", [D, 2 * FFN_H])
    wfo_d = din("w_ffn_out", [FFN_H, D])
    tabA_d = din("tab_a", [128, 2, S_LAT])
    tabB_d = din("tab_b", [128, 2, S_LAT])
    y_d = nc.dram_tensor("y", [NB, S_LAT, D], F32, kind="ExternalOutput").ap()

    with ExitStack() as es:
        S = Sched(nc, es)

        uid = [0]

        def sb(st, name, shape, dt):
            uid[0] += 1
            return st.enter_context(nc.sbuf_tensor("%s_u%d" % (name, uid[0]), list(shape), dt))

        def ring(st, name, shape, dt, n, nparts=1):
            return Ring([T(sb(st, "%s%d" % (name, i), shape, dt), nparts) for i in range(n)])

        banks = [T(es.enter_context(nc.psum_tensor("bk%d" % i, [128, 512], F32))) for i in range(8)]
        for bk in banks:
            bk.b.excl = True

        def bf(bank):
            return bank.t[:].bitcast(BF16)

        ident = T(sb(es, "ident", [128, 128], BF16))
        identf = T(sb(es, "identf", [128, 128], F32))
        ones_bf = T(sb(es, "ones_bf", [128, 128], BF16))
        ones_f = T(sb(es, "ones_f", [128, 128], F32))
        neghalf = T(sb(es, "neghalf", [128, 512], F32))
        smalls = T(sb(es, "smalls", [128, 320], F32))
        fnorm_bc = T(sb(es, "fnorm_bc", [128, D], F32))
        modfm = T(sb(es, "modfm", [128, 4, KC, 3], F32))
        G = T(sb(es, "G", [128, NB, 2, D], F32))
        nlam = T(sb(es, "nlam", [128, 1], F32))
        subln = T(sb(es, "subln", [128, 1], F32))
        C_CT, C_BFM, C_DL, C_SUB, C_QN, C_KVN = 0, 24, 56, 312, 313, 315

        with ExitStack() as ph:
            wada = T(sb(ph, "wada", [128, KC, 6 * D], BF16), 6)
            bada_g = T(sb(ph, "bada_g", [128, 2 * D], F32))
            scf = T(sb(ph, "scf", [128, 24], F32))
            scT = T(sb(ph, "scT", [128, 24], BF16))
            scB = T(sb(ph, "scB", [128, NB, KC, 128], BF16))
            lt = T(sb(ph, "lt", [128, 128], F32))
            ls = T(sb(ph, "ls", [128, 4], F32))

            S.add("sp", lambda e: e.dma_start(out=smalls.t[:], in_=smalls_d), writes=[smalls.b], dma=True)
            S.add("sp", lambda e: e.dma_start(out=bada_g.t[:], in_=bigs_d[:, 0:2 * D]), writes=[bada_g.b], dma=True)
            S.add("sp", lambda e: e.dma_start(out=fnorm_bc.t[:], in_=bigs_d[:, 2 * D:3 * D]), writes=[fnorm_bc.b], dma=True)
            wada_src = wada_d.rearrange("(c p) n -> p c n", p=128)
            for kind in (0, 1, 2, 3, 4, 5):
                S.add("pool", (lambda kind: lambda e: e.dma_start(
                    out=wada.t[:, :, kind * D:(kind + 1) * D], in_=wada_src[:, :, kind * D:(kind + 1) * D]))(kind),
                    writes=[wada.rb[kind]], dma=True)
            S.add("dve", lambda e: e.memset(identf.t[:], 0.0), writes=[identf.b])
            S.add("pool", lambda e: e.affine_select(out=identf.t[:], in_=identf.t[:], pattern=[[-1, 128]],
                                                    compare_op=ALU.not_equal, fill=1.0, base=0, channel_multiplier=1),
                  reads=[identf.b], writes=[identf.b])
            S.add("dve", lambda e: e.tensor_copy(out=ident.t[:], in_=identf.t[:]), reads=[identf.b], writes=[ident.b])
            S.add("dve", lambda e: e.memset(ones_bf.t[:], 1.0), writes=[ones_bf.b])
            S.add("dve", lambda e: e.memset(ones_f.t[:], 1.0), writes=[ones_f.b])
            S.add("dve", lambda e: e.memset(neghalf.t[:], -0.5), writes=[neghalf.b])
            S.add("dve", lambda e: e.tensor_tensor(out=lt.t[:, 0:64], in0=smalls.t[:, C_DL:C_DL + 64],
                                                   in1=smalls.t[:, C_DL + 64:C_DL + 128], op=ALU.mult),
                  reads=[smalls.b], writes=[lt.b])
            S.add("dve", lambda e: e.tensor_tensor(out=lt.t[:, 64:128], in0=smalls.t[:, C_DL + 128:C_DL + 192],
                                                   in1=smalls.t[:, C_DL + 192:C_DL + 256], op=ALU.mult),
                  reads=[smalls.b], writes=[lt.b])
            S.add("dve", lambda e: e.tensor_reduce(out=ls.t[:, 0:2], in_=lt.t[:].rearrange("p (a b) -> p a b", a=2),
                                                   axis=mybir.AxisListType.X, op=ALU.add),
                  reads=[lt.b], writes=[ls.b])
            S.add("act", lambda e: e.activation(out=ls.t[:, 2:4], in_=ls.t[:, 0:2], func=AF.Exp),
                  reads=[ls.b], writes=[ls.b])
            S.add("dve", lambda e: e.tensor_tensor(out=nlam.t[:], in0=ls.t[:, 3:4], in1=ls.t[:, 2:3], op=ALU.subtract),
                  reads=[ls.b], writes=[nlam.b])
            S.add("dve", lambda e: e.tensor_scalar(out=nlam.t[:], in0=nlam.t[:], scalar1=-LAM_INIT, scalar2=None,
                                                   op0=ALU.add), reads=[nlam.b], writes=[nlam.b])
            S.add("dve", lambda e: e.tensor_scalar(out=subln.t[:], in0=smalls.t[:, C_SUB:C_SUB + 1],
                                                   scalar1=1.0 - LAM_INIT, scalar2=None, op0=ALU.mult),
                  reads=[smalls.b], writes=[subln.b])
            S.add("act", lambda e: e.activation(out=scf.t[:], in_=smalls.t[:, C_CT:C_CT + 24], func=AF.Silu),
                  reads=[smalls.b], writes=[scf.b])
            S.add("dve", lambda e: e.tensor_copy(out=scT.t[:], in_=scf.t[:]), reads=[scf.b], writes=[scT.b])
            for j in range(NB):
                for c in range(KC):
                    S.add("dve", (lambda j, c: lambda e: e.tensor_scalar(
                        out=scB.t[:, j, c, :], in0=ones_bf.t[:], scalar1=scf.t[:, c * 3 + j:c * 3 + j + 1],
                        scalar2=None, op0=ALU.mult))(j, c), reads=[ones_bf.b, scf.b], writes=[scB.b])
            fm_bank = banks[0]
            kinds_fm = (0, 1, 3, 4)
            first = True
            for ki, kind in enumerate(kinds_fm):
                for m in range(KC):
                    col = (ki * KC + m) * 4
                    for c in range(KC):
                        S.add("pe", (lambda kind, m, c, col: lambda e: e.matmul(
                            fm_bank.t[:, col:col + 3],
                            lhsT=wada.t[:, c, kind * D + m * 128:kind * D + (m + 1) * 128],
                            rhs=scT.t[:, c * 3:c * 3 + 3], start=(c == 0), stop=(c == KC - 1)))(kind, m, c, col),
                            reads=[wada.rb[kind], scT.b], writes=[fm_bank.b])
            for j in range(3):
                S.add("dve", (lambda j: lambda e: e.tensor_tensor(
                    out=modfm.t[:, :, :, j].rearrange("p a b -> p (a b)"),
                    in0=fm_bank.t[:, 0:128].rearrange("p (a b) -> p a b", b=4)[:, :, j],
                    in1=smalls.t[:, C_BFM:C_BFM + 32], op=ALU.add))(j),
                    reads=[fm_bank.b, smalls.b], writes=[modfm.b])
            for ki in (1, 3):
                S.add("dve", (lambda ki: lambda e: e.tensor_scalar(
                    out=modfm.t[:, ki, :, :], in0=modfm.t[:, ki, :, :], scalar1=1.0, scalar2=None,
                    op0=ALU.add))(ki), reads=[modfm.b], writes=[modfm.b])
            bi = 1
            for j in range(NB):
                for gi, kind in enumerate((2, 5)):
                    for half in range(2):
                        bk = banks[1 + (bi % 7)]
                        bi += 1
                        for c in range(KC):
                            S.add("pe", (lambda j, kind, half, c, bk: lambda e: e.matmul(
                                bk.t[:, :], lhsT=scB.t[:, j, c, :],
                                rhs=wada.t[:, c, kind * D + half * 512:kind * D + (half + 1) * 512],
                                start=(c == 0), stop=(c == KC - 1)))(j, kind, half, c, bk),
                                reads=[scB.b, wada.rb[kind]], writes=[bk.b])
                        S.add("dve", (lambda j, gi, half, bk: lambda e: e.tensor_tensor(
                            out=G.t[:, j, gi, half * 512:(half + 1) * 512], in0=bk.t[:, :],
                            in1=bada_g.t[:, gi * D + half * 512:gi * D + (half + 1) * 512], op=ALU.add))(j, gi, half, bk),
                            reads=[bk.b, bada_g.b], writes=[G.b])
            S.emit()

        def h_block(st_pools, src_ap, ntiles, hT, jm, kind_sc, kind_sh, tpool, evac_engs, keep=None):
            xpool, xnpool, junk, st = st_pools

            def load_a(t):
                xt = xpool.next()
                S.add("sp", (lambda xt, t: lambda e: e.dma_start(out=xt.t[:], in_=src_ap[t * 128:(t + 1) * 128, :]))(xt, t),
                      writes=[xt.b], dma=True)
                S.rec_pad(2)
                return norm_a(st_pools, xt)

            if PIPE_H:
                a = load_a(0)
                for t in range(ntiles):
                    nxt = load_a(t + 1) if t + 1 < ntiles else None
                    trans_b(a, t, hT, jm, kind_sc, kind_sh, tpool, evac_engs)
                    a = nxt
            else:
                for t in range(ntiles):
                    a = load_a(t)
                    trans_b(a, t, hT, jm, kind_sc, kind_sh, tpool, evac_engs)

        XN_ENG = ["act"]

        def norm_a(st_pools, xt):
            xpool, xnpool, junk, st = st_pools
            s = st.next()
            xn = xnpool.next()
            S.add("dve", lambda e: e.scalar_tensor_tensor(out=junk.t[:], in0=xt.t[:], scalar=1.0, in1=xt.t[:],
                                                          op0=ALU.mult, op1=ALU.mult, accum_out=s.t[:, 0:1]),
                  reads=[xt.b], writes=[s.b])
            S.add("dve", lambda e: e.tensor_scalar(out=s.t[:, 1:2], in0=s.t[:, 0:1], scalar1=1.0 / D, scalar2=EPS,
                                                   op0=ALU.mult, op1=ALU.add), reads=[s.b], writes=[s.b])
            S.rec_yield()
            S.add("pool", lambda e: e.tensor_tensor(out=s.t[:, 2:3], in0=s.t[:, 1:2], in1=neghalf.t[:, 0:1], op=ALU.pow),
                  reads=[s.b], writes=[s.b])
            S.rec_yield()
            if XN_ENG[0] == "act":
                S.add("act", lambda e: e.activation(out=xn.t[:], in_=xt.t[:], func=AF.Copy, scale=s.t[:, 2:3]),
                      reads=[xt.b, s.b], writes=[xn.b])
            else:
                S.add("dve", lambda e: e.tensor_scalar(out=xn.t[:], in0=xt.t[:], scalar1=s.t[:, 2:3], scalar2=None,
                                                       op0=ALU.mult), reads=[xt.b, s.b], writes=[xn.b])
            S.rec_pad(3)
            return xn

        def trans_b(xn, t, hT, jm, kind_sc, kind_sh, tpool, evac_engs):
            tbs = tpool.next()
            nb_ = len(tbs)
            per = KC // nb_
            for c in range(KC):
                tb = tbs[c // per]
                tbf = bf(tb)
                cc = c % per
                S.add("pe", (lambda c, cc, tbf: lambda e: e.transpose(tbf[:, cc * 128:(cc + 1) * 128],
                                                                     xn.t[:, c * 128:(c + 1) * 128], ident.t[:]))(c, cc, tbf),
                      reads=[xn.b, ident.b], writes=[tb.b])
            S.rec_pad(1)
            for c in range(KC):
                tb = tbs[c // per]
                tbf = bf(tb)
                cc = c % per
                eng = "dve" if (nb_ == 1 or c // per == 0) else "act"
                sc_ap = modfm.t[:, kind_sc, c, jm:jm + 1]
                sh_ap = modfm.t[:, kind_sh, c, jm:jm + 1]
                if eng == "act":
                    S.add("act", (lambda c, cc, tbf, sc_ap, sh_ap: lambda e: e.activation(
                        out=hT.t[:, c, t * 128:(t + 1) * 128], in_=tbf[:, cc * 128:(cc + 1) * 128], func=AF.Identity,
                        scale=sc_ap, bias=sh_ap))(c, cc, tbf, sc_ap, sh_ap), reads=[tb.b, modfm.b], writes=[hT.rb[c]])
                else:
                    S.add("dve", (lambda c, cc, tbf, sc_ap, sh_ap: lambda e: e.tensor_scalar(
                        out=hT.t[:, c, t * 128:(t + 1) * 128], in0=tbf[:, cc * 128:(cc + 1) * 128], scalar1=sc_ap,
                        scalar2=sh_ap, op0=ALU.mult, op1=ALU.add))(c, cc, tbf, sc_ap, sh_ap),
                        reads=[tb.b, modfm.b], writes=[hT.rb[c]])
            S.rec_yield()

        def proj_fm(bank, w, col0, M, hT, N, nk=KC, rhs3=True):
            for c in range(nk):
                S.add("pe", (lambda c: lambda e: e.matmul(
                    bank.t[0:M, 0:N], lhsT=w.t[:, c, col0:col0 + M], rhs=hT.t[:, c, 0:N],
                    start=(c == 0), stop=(c == nk - 1)))(c), reads=w.rb + [hT.rb[c]], writes=[bank.b])
            S.rec_yield()
            S.tick(2)

        def attention(groups, spool, obanks, ppool, bg=None, every=2):
            LA = len(spool.items) - 1
            bg = list(bg) if bg else []
            it_no = [0]
            its = []
            for g in groups:
                for kc in range(NKC):
                    its.append((g, kc))
            pend = []

            def do_av(g, kc, pt):
                ob = g["ob"]
                for t in range(4):
                    bk, c0 = ob[t]
                    first = (kc == 0) and all(ob[u][0] is not bk for u in range(t))
                    S.add("pe", (lambda t, bk, c0, first: lambda e: e.matmul(
                        bk.t[:, c0:c0 + g["ncols"]], lhsT=pt.t[:, t * 128:(t + 1) * 128], rhs=g["v"](kc),
                        start=first, stop=(kc == NKC - 1), skip_group_check=True))(t, bk, c0, first),
                        reads=[pt.b] + g["vreads"], writes=[bk.b])
                if kc == NKC - 1:
                    g["post"]()

            for (g, kc) in its:
                sbk = spool.next()
                S.add("pe", (lambda g, kc, sbk: lambda e: e.matmul(
                    sbk.t[:, :], lhsT=g["kT"](kc), rhs=g["qT"], start=True, stop=True))(g, kc, sbk),
                    reads=g["qkreads"], writes=[sbk.b])
                pt = ppool.next()
                S.add("act", (lambda g, sbk, pt: lambda e: e.activation(
                    out=pt.t[:], in_=sbk.t[:, :], func=AF.Exp, scale=g["scale"]))(g, sbk, pt),
                    reads=[sbk.b], writes=[pt.b])
                pend.append((g, kc, pt))
                if len(pend) > LA:
                    do_av(*pend.pop(0))
                it_no[0] += 1
                if bg and it_no[0] % every == 0:
                    S.replay(bg.pop(0))
            while pend:
                do_av(*pend.pop(0))
            while bg:
                S.replay(bg.pop(0))

        for j in range(NB):
            with ExitStack() as pb:
                catT = T(sb(pb, "catT", [128, KC, S_LAT], BF16))

                with ExitStack() as pd:
                    kAT = T(sb(pd, "kAT", [128, 4, NKC * 128], BF16))
                    vA = T(sb(pd, "vA", [128, NKC, 4 * 129], BF16))

                    with ExitStack() as ph:
                        w = T(sb(ph, "wdk", [128, KC, 1536], BF16), 3)
                        hpool = ring(ph, "hT", [128, KC, 512], BF16, 2, KC)
                        xpool = ring(ph, "xt", [128, D], F32, 2)
                        xnpool = ring(ph, "xn", [128, D], BF16, 2)
                        junk = T(sb(ph, "junk", [128, D], BF16))
                        stp = ring(ph, "st", [128, 4], F32, 4)
                        tabp = ring(ph, "tab", [128, 2, 512], F32, 2)
                        t1p = ring(ph, "t1", [128, 512], F32, 2)
                        t2p = ring(ph, "t2", [128, 512], F32, 2)
                        pools = (xpool, xnpool, junk, stp)
                        tpool = Ring([(banks[4], banks[5]), (banks[6], banks[7])])
                        ppool = Ring(banks[0:4])
                        wsrc = wD_d.rearrange("(c p) n -> p c n", p=128)
                        for part in range(3):
                            S.add("pool", (lambda part: lambda e: e.dma_start(
                                out=w.t[:, :, part * 512:(part + 1) * 512], in_=wsrc[:, :, part * 512:(part + 1) * 512]))(part),
                                writes=[w.rb[part]], dma=True)
                        S.add("pool", lambda e: e.memset(vA.t[:], 1.0), writes=[vA.b])
                        def hb_d(blk):
                            lat = blk > 0
                            ntiles = 4 if lat else 2
                            src = x_d[j, (blk - 1) * 512:blk * 512, :] if lat else ctx_d[j]
                            hT = hpool.next()
                            tab = None
                            if lat:
                                tab = tabp.next()
                                S.add("sp", (lambda tab, blk: lambda e: e.dma_start(
                                    out=tab.t[:], in_=tabA_d[:, :, (blk - 1) * 512:blk * 512]))(tab, blk),
                                    writes=[tab.b], dma=True)
                            h_block(pools, src, ntiles, hT, j if lat else 2, 1, 0, tpool, ("dve", "act"))
                            return hT, tab

                        cur = hb_d(0)
                        for blk in range(5):
                            lat = blk > 0
                            ntiles = 4 if lat else 2
                            N = ntiles * 128
                            key0 = 0 if not lat else S_CTX + (blk - 1) * 512
                            hT, tab = cur
                            if blk < 4:
                                S.rec_begin()
                                cur = hb_d(blk + 1)
                                S.bg = S.rec_end()
                            for m in range(4):
                                ba = ppool.next()
                                proj_fm(ba, w, m * 128, 128, hT, N)
                                if lat:
                                    bb = ppool.next()
                                    proj_fm(bb, w, 512 + m * 128, 128, hT, N)
                                    t1 = t1p.next()
                                    t2 = t2p.next()
                                    S.add("dve", (lambda ba, t1, tab: lambda e: e.tensor_tensor(
                                        out=t1.t[:], in0=ba.t[:, :], in1=tab.t[:, 0, :], op=ALU.mult))(ba, t1, tab),
                                        reads=[ba.b, tab.b], writes=[t1.b])
                                    S.add("dve", (lambda bb, t2, tab: lambda e: e.tensor_tensor(
                                        out=t2.t[:], in0=bb.t[:, :], in1=tab.t[:, 1, :], op=ALU.mult))(bb, t2, tab),
                                        reads=[bb.b, tab.b], writes=[t2.b])
                                    S.add("pool", (lambda m, t1, t2, key0: lambda e: e.tensor_tensor(
                                        out=kAT.t[:, m, key0:key0 + 512], in0=t1.t[:], in1=t2.t[:], op=ALU.add))(m, t1, t2, key0),
                                        reads=[t1.b, t2.b], writes=[kAT.b])
                                else:
                                    S.add("act", (lambda m, ba, N: lambda e: e.activation(
                                        out=kAT.t[:, m, 0:N], in_=ba.t[:, 0:N], func=AF.Copy))(m, ba, N),
                                        reads=[ba.b], writes=[kAT.b])
                            for t in range(ntiles):
                                bv = ppool.next()
                                for c in range(KC):
                                    S.add("pe", (lambda t, c, bv, hT: lambda e: e.matmul(
                                        bv.t[:, :], lhsT=hT.t[:, c, t * 128:(t + 1) * 128], rhs=w.t[:, c, 1024:1536],
                                        start=(c == 0), stop=(c == KC - 1)))(t, c, bv, hT),
                                        reads=[hT.rb[c]] + w.rb, writes=[bv.b])
                                kc = key0 // 128 + t
                                S.add("act", (lambda kc, bv: lambda e: e.activation(
                                    out=vA.t[:, kc, :].rearrange("p (h e) -> p h e", e=129)[:, :, 0:128],
                                    in_=bv.t[:, :].rearrange("p (h e) -> p h e", e=128), func=AF.Copy))(kc, bv),
                                    reads=[bv.b], writes=[vA.b])
                                S.tick(2)
                            S.flush_bg()
                        S.emit()

                    with ExitStack() as ph:
                        w = T(sb(ph, "wdq", [128, KC, 1024], BF16), 2)
                        hpool = ring(ph, "hT", [128, KC, 512], BF16, 1, KC)
                        xpool = ring(ph, "xt", [128, D], F32, 2)
                        xnpool = ring(ph, "xn", [128, D], BF16, 2)
                        junk = T(sb(ph, "junk", [128, D], BF16))
                        stp = ring(ph, "st", [128, 4], F32, 4)
                        tabp = ring(ph, "tab", [128, 2, 512], F32, 2)
                        t1p = ring(ph, "t1", [128, 512], F32, 2)
                        t2p = ring(ph, "t2", [128, 512], F32, 2)
                        qpool = ring(ph, "qAT", [128, 8, 512], BF16, 2)
                        for qz in qpool.items:
                            S.add("pool", (lambda qz: lambda e: e.memset(qz.t[:], 0.0))(qz), writes=[qz.b])
                        ptp = ring(ph, "pt", [128, 512], BF16, 4)
                        orawp = ring(ph, "oraw", [128, 4, 132], F32, 3)
                        rp = ring(ph, "rr", [128, 8], F32, 4)
                        t1op = ring(ph, "t1o", [128, 4, 128], F32, 2)
                        oap = ring(ph, "oa", [128, 4, 128], F32, 2)
                        ssp = ring(ph, "ssq", [128, 16], F32, 2)
                        oanp = ring(ph, "oan", [128, 4, 512], BF16, 2)
                        junk2 = T(sb(ph, "junk2", [128, 128], BF16))
                        pools = (xpool, xnpool, junk, stp)
                        tpool = Ring([(banks[7],)])
                        ppool = Ring([banks[7]])
                        spool = Ring(banks[0:3])
                        obanks = banks[3:7]
                        wsrc = wD_d.rearrange("(c p) n -> p c n", p=128)
                        for part in range(2):
                            S.add("pool", (lambda part: lambda e: e.dma_start(
                                out=w.t[:, :, part * 512:(part + 1) * 512],
                                in_=wsrc[:, :, 1536 + part * 512:1536 + (part + 1) * 512]))(part),
                                writes=[w.rb[part]], dma=True)
                        def prep_d(qb):
                            hT = hpool.next()
                            tab = tabp.next()
                            S.add("sp", (lambda tab, qb: lambda e: e.dma_start(
                                out=tab.t[:], in_=tabA_d[:, :, qb * 512:(qb + 1) * 512]))(tab, qb),
                                writes=[tab.b], dma=True)
                            h_block(pools, x_d[j, qb * 512:(qb + 1) * 512, :], 4, hT, j, 1, 0, tpool, ("dve",))
                            qAT = qpool.next()
                            for m in range(4):
                                ba = ppool.next()
                                proj_fm(ba, w, m * 128, 128, hT, 512)
                                t1 = t1p.next()
                                t2 = t2p.next()
                                S.add("dve", (lambda ba, t1, tab: lambda e: e.tensor_tensor(
                                    out=t1.t[:], in0=ba.t[:, :], in1=tab.t[:, 0, :], op=ALU.mult))(ba, t1, tab),
                                    reads=[ba.b, tab.b], writes=[t1.b])
                                S.rec_yield()
                                bb = ppool.next()
                                proj_fm(bb, w, 512 + m * 128, 128, hT, 512)
                                S.add("dve", (lambda bb, t2, tab: lambda e: e.tensor_tensor(
                                    out=t2.t[:], in0=bb.t[:, :], in1=tab.t[:, 1, :], op=ALU.mult))(bb, t2, tab),
                                    reads=[bb.b, tab.b], writes=[t2.b])
                                S.rec_yield()
                                for mp in range(2):
                                    S.add("pool", (lambda m, mp, t1, t2, qAT: lambda e: e.tensor_tensor(
                                        out=qAT.t[64 * mp:64 * mp + 64, 2 * m + mp, :], in0=t1.t[64 * mp:64 * mp + 64, :],
                                        in1=t2.t[64 * mp:64 * mp + 64, :], op=ALU.add))(m, mp, t1, t2, qAT),
                                        reads=[t1.b, t2.b], writes=[qAT.b])
                                S.rec_yield()
                            return qAT

                        XN_ENG[0] = "dve"
                        S.pad_on = True
                        qAT_next = prep_d(0)
                        tail_bg = []
                        for qb in range(4):
                            qAT = qAT_next
                            bg = None
                            if qb < 3:
                                S.rec_begin()
                                qAT_next = prep_d(qb + 1)
                                bg = S.rec_end()
                            oan = oanp.next()
                            groups = []
                            state = {}
                            obsets = [[(banks[3], 0), (banks[3], 129), (banks[4], 0), (banks[4], 129)],
                                      [(banks[5], 0), (banks[5], 129), (banks[6], 0), (banks[6], 129)]]
                            for h in range(4):
                                for mp in range(2):
                                    ob = obsets[mp]

                                    def post(h=h, mp=mp, oan=oan, ob=ob):
                                        oraw = orawp.next()
                                        for t in range(4):
                                            S.add("dve", (lambda t, oraw: lambda e: e.tensor_copy(
                                                out=oraw.t[:, t, 0:129], in_=ob[t][0].t[:, ob[t][1]:ob[t][1] + 129]))(t, oraw),
                                                reads=[ob[t][0].b], writes=[oraw.b])
                                        r = rp.next()
                                        S.add("dve", lambda e: e.reciprocal(out=r.t[:, 0:4], in_=oraw.t[:, :, 128]),
                                              reads=[oraw.b], writes=[r.b])
                                        if mp == 0:
                                            state["oraw0"] = oraw
                                            state["r0"] = r
                                            return
                                        oraw0, r0 = state["oraw0"], state["r0"]
                                        S.add("dve", lambda e: e.tensor_scalar(out=r.t[:, 4:8], in0=r.t[:, 0:4],
                                                                               scalar1=nlam.t[:, 0:1], scalar2=None,
                                                                               op0=ALU.mult),
                                              reads=[r.b, nlam.b], writes=[r.b])
                                        t1o = t1op.next()
                                        oa = oap.next()
                                        ss = ssp.next()
                                        for t in range(4):
                                            S.add("dve", (lambda t: lambda e: e.tensor_scalar(
                                                out=t1o.t[:, t, :], in0=oraw0.t[:, t, 0:128], scalar1=r0.t[:, t:t + 1],
                                                scalar2=None, op0=ALU.mult))(t),
                                                reads=[oraw0.b, r0.b], writes=[t1o.b])
                                            S.add("dve", (lambda t: lambda e: e.scalar_tensor_tensor(
                                                out=oa.t[:, t, :], in0=oraw.t[:, t, 0:128], scalar=r.t[:, 4 + t:5 + t],
                                                in1=t1o.t[:, t, :], op0=ALU.mult, op1=ALU.add))(t),
                                                reads=[oraw.b, r.b, t1o.b], writes=[oa.b])
                                            S.add("dve", (lambda t: lambda e: e.scalar_tensor_tensor(
                                                out=junk2.t[:], in0=oa.t[:, t, :], scalar=1.0, in1=oa.t[:, t, :],
                                                op0=ALU.mult, op1=ALU.mult, accum_out=ss.t[:, t:t + 1]))(t),
                                                reads=[oa.b], writes=[ss.b])
                                        S.add("dve", lambda e: e.tensor_scalar(out=ss.t[:, 4:8], in0=ss.t[:, 0:4],
                                                                               scalar1=1.0 / 128, scalar2=EPS,
                                                                               op0=ALU.mult, op1=ALU.add),
                                              reads=[ss.b], writes=[ss.b])
                                        S.add("pool", lambda e: e.tensor_tensor(out=ss.t[:, 8:12], in0=ss.t[:, 4:8],
                                                                                in1=neghalf.t[:, 0:4], op=ALU.pow),
                                              reads=[ss.b, neghalf.b], writes=[ss.b])
                                        for t in range(4):
                                            S.add("dve", (lambda t: lambda e: e.tensor_scalar(
                                                out=oan.t[:, t, h * 128:(h + 1) * 128], in0=oa.t[:, t, :],
                                                scalar1=ss.t[:, 8 + t:9 + t], scalar2=None, op0=ALU.mult))(t),
                                                reads=[oa.b, ss.b], writes=[oan.b])

                                    base = 64 * mp
                                    groups.append(dict(
                                        qT=qAT.t[:, 2 * h + mp, :],
                                        kT=(lambda kc, h=h: kAT.t[:, h, kc * 128:(kc + 1) * 128]),
                                        v=(lambda kc, h=h: vA.t[:, kc, h * 129:(h + 1) * 129]),
                                        ncols=129, scale=0.125, qkreads=[qAT.b, kAT.b], vreads=[vA.b], post=post, ob=ob))
                            attention(groups, spool, obanks, ptp, bg=(tail_bg + (bg or [])), every=1)
                            S.rec_begin()
                            S.rec_pad(8)
                            for t in range(4):
                                tb = tpool.next()[0]
                                tbf = bf(tb)
                                for h in range(4):
                                    S.add("pe", (lambda t, h, tbf, oan: lambda e: e.transpose(
                                        tbf[:, h * 128:(h + 1) * 128], oan.t[:, t, h * 128:(h + 1) * 128], ident.t[:]))(t, h, tbf, oan),
                                        reads=[oan.b, ident.b], writes=[tb.b])
                                q0 = qb * 512 + t * 128
                                S.add("dve", (lambda tbf, q0: lambda e: e.tensor_scalar(
                                    out=catT.t[:, 0:4, q0:q0 + 128], in0=tbf[:, 0:512].rearrange("p (h q) -> p h q", h=4),
                                    scalar1=subln.t[:, 0:1], scalar2=None, op0=ALU.mult))(tbf, q0),
                                    reads=[tb.b, subln.b], writes=[catT.b])
                                S.rec_yield()
                            tail_bg = S.rec_end()
                        for g_ in tail_bg:
                            S.replay(g_)
                        XN_ENG[0] = "act"
                        S.pad_on = False
                        S.emit()

                with ExitStack() as pm:
                    KbT = T(sb(pm, "KbT", [128, 8, NKC * 128], BF16))
                    vB = T(sb(pm, "vB", [128, NKC, 8 * 65], BF16))

                    with ExitStack() as ph:
                        w = T(sb(ph, "wmk", [128, KC, 192], BF16))
                        wkvk = T(sb(ph, "wkvk", [128, 512], BF16))
                        wkvv = T(sb(ph, "wkvv", [128, 512], BF16))
                        hpool = ring(ph, "hT", [128, KC, 512], BF16, 2, KC)
                        xpool = ring(ph, "xt", [128, D], F32, 2)
                        xnpool = ring(ph, "xn", [128, D], BF16, 2)
                        junk = T(sb(ph, "junk", [128, D], BF16))
                        stp = ring(ph, "st", [128, 4], F32, 4)
                        tabp = ring(ph, "tab", [128, 2, 512], F32, 2)
                        t1p = ring(ph, "t1", [128, 512], F32, 2)
                        t2p = ring(ph, "t2", [128, 512], F32, 2)
                        sqp = ring(ph, "sq", [128, 512], F32, 2)
                        rsp = ring(ph, "rs", [128, 512], F32, 2)
                        ckvnp = ring(ph, "ckvn", [128, 512], BF16, 2)
                        krp = ring(ph, "kr", [128, 512], BF16, 2)
                        pools = (xpool, xnpool, junk, stp)
                        tpool = Ring([(banks[4], banks[5]), (banks[6], banks[7])])
                        ppool = Ring(banks[0:4])
                        S.add("pool", lambda e: e.dma_start(out=w.t[:], in_=wM_d.rearrange("(c p) n -> p c n", p=128)[:, :, 0:192]),
                              writes=[w.b], dma=True)
                        S.add("pool", lambda e: e.dma_start(out=wkvk.t[:], in_=wkvk_d), writes=[wkvk.b], dma=True)
                        S.add("pool", lambda e: e.dma_start(out=wkvv.t[:], in_=wkvv_d), writes=[wkvv.b], dma=True)
                        S.add("pool", lambda e: e.memset(vB.t[:], 1.0), writes=[vB.b])
                        def hb_m(blk):
                            lat = blk > 0
                            ntiles = 4 if lat else 2
                            src = x_d[j, (blk - 1) * 512:blk * 512, :] if lat else ctx_d[j]
                            hT = hpool.next()
                            tab = None
                            if lat:
                                tab = tabp.next()
                                S.add("sp", (lambda tab, blk: lambda e: e.dma_start(
                                    out=tab.t[:], in_=tabB_d[:, :, (blk - 1) * 512:blk * 512]))(tab, blk),
                                    writes=[tab.b], dma=True)
                            h_block(pools, src, ntiles, hT, j if lat else 2, 1, 0, tpool, ("dve", "act"))
                            return hT, tab

                        cur = hb_m(0)
                        for blk in range(5):
                            lat = blk > 0
                            ntiles = 4 if lat else 2
                            N = ntiles * 128
                            key0 = 0 if not lat else S_CTX + (blk - 1) * 512
                            hT, tab = cur
                            if blk < 4:
                                S.rec_begin()
                                cur = hb_m(blk + 1)
                                S.bg = S.rec_end()
                            bc = ppool.next()
                            proj_fm(bc, w, 0, 128, hT, N)
                            br = ppool.next()
                            proj_fm(br, w, 128, 32, hT, N)
                            kr = krp.next()
                            if lat:
                                brp = ppool.next()
                                proj_fm(brp, w, 160, 32, hT, N)
                                t1 = t1p.next()
                                t2 = t2p.next()
                                S.add("dve", (lambda br, t1, tab: lambda e: e.tensor_tensor(
                                    out=t1.t[0:32, :], in0=br.t[0:32, :], in1=tab.t[0:32, 0, :], op=ALU.mult))(br, t1, tab),
                                    reads=[br.b, tab.b], writes=[t1.b])
                                S.add("dve", (lambda brp, t2, tab: lambda e: e.tensor_tensor(
                                    out=t2.t[0:32, :], in0=brp.t[0:32, :], in1=tab.t[0:32, 1, :], op=ALU.mult))(brp, t2, tab),
                                    reads=[brp.b, tab.b], writes=[t2.b])
                                S.add("pool", (lambda t1, t2, kr: lambda e: e.tensor_tensor(
                                    out=kr.t[0:32, :], in0=t1.t[0:32, :], in1=t2.t[0:32, :], op=ALU.add))(t1, t2, kr),
                                    reads=[t1.b, t2.b], writes=[kr.b])
                            else:
                                S.add("act", (lambda br, kr, N: lambda e: e.activation(
                                    out=kr.t[0:32, 0:N], in_=br.t[0:32, 0:N], func=AF.Copy))(br, kr, N),
                                    reads=[br.b], writes=[kr.b])
                            S.add("act", (lambda kr, N, key0: lambda e: e.activation(
                                out=KbT.t[64:96, :, key0:key0 + N],
                                in_=kr.t[0:32, 0:N].unsqueeze(1).to_broadcast([32, 8, N]), func=AF.Copy))(kr, N, key0),
                                reads=[kr.b], writes=[KbT.b])
                            sq = sqp.next()
                            S.add("act", (lambda sq, bc, N: lambda e: e.activation(
                                out=sq.t[:, 0:N], in_=bc.t[:, 0:N], func=AF.Square))(sq, bc, N),
                                reads=[bc.b], writes=[sq.b])
                            bs = ppool.next()
                            S.add("pe", (lambda bs, sq, N: lambda e: e.matmul(
                                bs.t[:, 0:N], lhsT=ones_f.t[:], rhs=sq.t[:, 0:N], start=True, stop=True))(bs, sq, N),
                                reads=[ones_f.b, sq.b], writes=[bs.b])
                            rs = rsp.next()
                            S.add("dve", (lambda rs, bs, N: lambda e: e.tensor_scalar(
                                out=rs.t[:, 0:N], in0=bs.t[:, 0:N], scalar1=1.0 / 128, scalar2=EPS,
                                op0=ALU.mult, op1=ALU.add))(rs, bs, N), reads=[bs.b], writes=[rs.b])
                            S.add("act", (lambda rs, N: lambda e: e.activation(
                                out=rs.t[:, 0:N], in_=rs.t[:, 0:N], func=AF.Ln))(rs, N), reads=[rs.b], writes=[rs.b])
                            S.add("act", (lambda rs, N: lambda e: e.activation(
                                out=rs.t[:, 0:N], in_=rs.t[:, 0:N], func=AF.Exp, scale=-0.5))(rs, N),
                                reads=[rs.b], writes=[rs.b])
                            ckvn = ckvnp.next()
                            S.add("dve", (lambda ckvn, bc, rs, N: lambda e: e.scalar_tensor_tensor(
                                out=ckvn.t[:, 0:N], in0=bc.t[:, 0:N], scalar=smalls.t[:, C_KVN:C_KVN + 1],
                                in1=rs.t[:, 0:N], op0=ALU.mult, op1=ALU.mult))(ckvn, bc, rs, N),
                                reads=[bc.b, rs.b, smalls.b], writes=[ckvn.b])
                            for pj in range(4):
                                bk_ = ppool.next()
                                S.add("pe", (lambda pj, bk_, ckvn, N: lambda e: e.matmul(
                                    bk_.t[:, 0:N], lhsT=wkvk.t[:, pj * 128:(pj + 1) * 128], rhs=ckvn.t[:, 0:N],
                                    start=True, stop=True))(pj, bk_, ckvn, N),
                                    reads=[wkvk.b, ckvn.b], writes=[bk_.b])
                                S.add("act", (lambda pj, bk_, N, key0: lambda e: e.activation(
                                    out=KbT.t[0:64, 2 * pj, key0:key0 + N], in_=bk_.t[0:64, 0:N], func=AF.Copy))(pj, bk_, N, key0),
                                    reads=[bk_.b], writes=[KbT.b])
                                S.add("dve", (lambda pj, bk_, N, key0: lambda e: e.tensor_copy(
                                    out=KbT.t[0:64, 2 * pj + 1, key0:key0 + N], in_=bk_.t[64:128, 0:N]))(pj, bk_, N, key0),
                                    reads=[bk_.b], writes=[KbT.b])
                                S.tick(2)
                            for t in range(ntiles):
                                bv = ppool.next()
                                S.add("pe", (lambda t, bv, ckvn: lambda e: e.matmul(
                                    bv.t[:, :], lhsT=ckvn.t[:, t * 128:(t + 1) * 128], rhs=wkvv.t[:, :],
                                    start=True, stop=True))(t, bv, ckvn),
                                    reads=[ckvn.b, wkvv.b], writes=[bv.b])
                                kc = key0 // 128 + t
                                S.add("act", (lambda kc, bv: lambda e: e.activation(
                                    out=vB.t[:, kc, :].rearrange("p (h e) -> p h e", e=65)[:, :, 0:64],
                                    in_=bv.t[:, :].rearrange("p (h e) -> p h e", e=64), func=AF.Copy))(kc, bv),
                                    reads=[bv.b], writes=[vB.b])
                                S.tick(3)
                            S.flush_bg()
                        S.emit()

                    with ExitStack() as ph:
                        w = T(sb(ph, "wmq", [128, KC, 256], BF16))
                        wqup = T(sb(ph, "wqup", [128, 2, 768], BF16))
                        wqupp = T(sb(ph, "wqupp", [128, 2, 256], BF16))
                        hpool = ring(ph, "hT", [128, KC, 512], BF16, 1, KC)
                        xpool = ring(ph, "xt", [128, D], F32, 2)
                        xnpool = ring(ph, "xn", [128, D], BF16, 2)
                        junk = T(sb(ph, "junk", [128, D], BF16))
                        stp = ring(ph, "st", [128, 4], F32, 4)
                        tabp = ring(ph, "tab", [128, 2, 512], F32, 2)
                        t1p = ring(ph, "t1", [128, 512], F32, 2)
                        t2p = ring(ph, "t2", [128, 512], F32, 2)
                        sqp = ring(ph, "sq", [128, 2, 512], F32, 1)
                        cqrawp = ring(ph, "cqraw", [128, 2, 512], F32, 1)
                        rsp = ring(ph, "rs", [128, 512], F32, 1)
                        cqnp = ring(ph, "cqn", [128, 2, 512], BF16, 1)
                        qpool = ring(ph, "qbT", [128, 8, 512], BF16, 2)
                        ptp = ring(ph, "pt", [128, 512], BF16, 4)
                        orawp = ring(ph, "oraw", [128, 4, 68], F32, 3)
                        rp = ring(ph, "rr", [128, 4], F32, 4)
                        obnp = ring(ph, "obn", [128, 4, 512], BF16, 2)
                        pools = (xpool, xnpool, junk, stp)
                        tpool = Ring([(banks[7],)])
                        ppool = Ring([banks[7]])
                        spool = Ring(banks[0:3])
                        obanks = banks[3:7]
                        S.add("pool", lambda e: e.dma_start(out=w.t[:], in_=wM_d.rearrange("(c p) n -> p c n", p=128)[:, :, 192:448]),
                              writes=[w.b], dma=True)
                        S.add("pool", lambda e: e.dma_start(out=wqup.t[:], in_=wqup_d.rearrange("(c p) n -> p c n", p=128)),
                              writes=[wqup.b], dma=True)
                        S.add("pool", lambda e: e.dma_start(out=wqupp.t[:], in_=wqupp_d.rearrange("(c p) n -> p c n", p=128)),
                              writes=[wqupp.b], dma=True)
                        def prep_m(qb):
                            hT = hpool.next()
                            tab = tabp.next()
                            S.add("sp", (lambda tab, qb: lambda e: e.dma_start(
                                out=tab.t[:], in_=tabB_d[:, :, qb * 512:(qb + 1) * 512]))(tab, qb),
                                writes=[tab.b], dma=True)
                            h_block(pools, x_d[j, qb * 512:(qb + 1) * 512, :], 4, hT, j, 1, 0, tpool, ("dve",))
                            cqraw = cqrawp.next()
                            sq = sqp.next()
                            for mc in range(2):
                                bq = ppool.next()
                                proj_fm(bq, w, mc * 128, 128, hT, 512)
                                S.add("dve", (lambda mc, cqraw, bq: lambda e: e.tensor_copy(
                                    out=cqraw.t[:, mc, :], in_=bq.t[:, :]))(mc, cqraw, bq),
                                    reads=[bq.b], writes=[cqraw.b])
                                S.rec_yield()
                                S.add("dve", (lambda mc, sq, cqraw: lambda e: e.tensor_tensor(
                                    out=sq.t[:, mc, :], in0=cqraw.t[:, mc, :], in1=cqraw.t[:, mc, :], op=ALU.mult))(mc, sq, cqraw),
                                    reads=[cqraw.b], writes=[sq.b])
                                S.rec_yield()
                            bs = ppool.next()
                            for mc in range(2):
                                S.add("pe", (lambda mc, bs, sq: lambda e: e.matmul(
                                    bs.t[:, :], lhsT=ones_f.t[:], rhs=sq.t[:, mc, :], start=(mc == 0), stop=(mc == 1)))(mc, bs, sq),
                                    reads=[ones_f.b, sq.b], writes=[bs.b])
                            S.rec_yield()
                            rs = rsp.next()
                            S.add("dve", (lambda rs, bs: lambda e: e.tensor_scalar(
                                out=rs.t[:], in0=bs.t[:, :], scalar1=1.0 / 256, scalar2=EPS, op0=ALU.mult, op1=ALU.add))(rs, bs),
                                reads=[bs.b], writes=[rs.b])
                            S.rec_pad(3)
                            S.add("act", (lambda rs: lambda e: e.activation(
                                out=rs.t[:], in_=rs.t[:], func=AF.Ln))(rs), reads=[rs.b], writes=[rs.b])
                            S.add("act", (lambda rs: lambda e: e.activation(
                                out=rs.t[:], in_=rs.t[:], func=AF.Exp, scale=-0.5))(rs), reads=[rs.b], writes=[rs.b])
                            S.rec_yield()
                            cqn = cqnp.next()
                            for mc in range(2):
                                S.add("dve", (lambda mc, cqn, cqraw, rs: lambda e: e.scalar_tensor_tensor(
                                    out=cqn.t[:, mc, :], in0=cqraw.t[:, mc, :], scalar=smalls.t[:, C_QN + mc:C_QN + mc + 1],
                                    in1=rs.t[:], op0=ALU.mult, op1=ALU.mult))(mc, cqn, cqraw, rs),
                                    reads=[cqraw.b, rs.b, smalls.b], writes=[cqn.b])
                            S.rec_yield()
                            qbT = qpool.next()
                            for h in range(8):
                                bp = ppool.next()
                                for kc2 in range(2):
                                    S.add("pe", (lambda h, kc2, bp, cqn: lambda e: e.matmul(
                                        bp.t[0:96, :], lhsT=wqup.t[:, kc2, h * 96:(h + 1) * 96], rhs=cqn.t[:, kc2, :],
                                        start=(kc2 == 0), stop=(kc2 == 1)))(h, kc2, bp, cqn),
                                        reads=[wqup.b, cqn.b], writes=[bp.b])
                                S.rec_yield()
                                S.add("dve", (lambda h, bp, qbT: lambda e: e.tensor_copy(
                                    out=qbT.t[0:64, h, :], in_=bp.t[0:64, :]))(h, bp, qbT),
                                    reads=[bp.b], writes=[qbT.b])
                                t1 = t1p.next()
                                t2 = t2p.next()
                                S.add("dve", (lambda bp, t1, tab: lambda e: e.tensor_tensor(
                                    out=t1.t[64:96, :], in0=bp.t[64:96, :], in1=tab.t[64:96, 0, :], op=ALU.mult))(bp, t1, tab),
                                    reads=[bp.b, tab.b], writes=[t1.b])
                                S.rec_yield()
                                bpp = ppool.next()
                                for kc2 in range(2):
                                    S.add("pe", (lambda h, kc2, bpp, cqn: lambda e: e.matmul(
                                        bpp.t[0:32, :], lhsT=wqupp.t[:, kc2, h * 32:(h + 1) * 32], rhs=cqn.t[:, kc2, :],
                                        start=(kc2 == 0), stop=(kc2 == 1)))(h, kc2, bpp, cqn),
                                        reads=[wqupp.b, cqn.b], writes=[bpp.b])
                                S.rec_yield()
                                S.add("dve", (lambda bpp, t2, tab: lambda e: e.tensor_tensor(
                                    out=t2.t[64:96, :], in0=bpp.t[0:32, :], in1=tab.t[64:96, 1, :], op=ALU.mult))(bpp, t2, tab),
                                    reads=[bpp.b, tab.b], writes=[t2.b])
                                S.add("pool", (lambda h, t1, t2, qbT: lambda e: e.tensor_tensor(
                                    out=qbT.t[64:96, h, :], in0=t1.t[64:96, :], in1=t2.t[64:96, :], op=ALU.add))(h, t1, t2, qbT),
                                    reads=[t1.b, t2.b], writes=[qbT.b])
                                S.rec_yield()
                            return qbT

                        XN_ENG[0] = "dve"
                        S.pad_on = True
                        qbT_next = prep_m(0)
                        tail_bg = []
                        for qb in range(4):
                            qbT = qbT_next
                            bg = None
                            if qb < 3:
                                S.rec_begin()
                                qbT_next = prep_m(qb + 1)
                                bg = S.rec_end()
                            obn = obnp.next()
                            groups = []
                            obsets = [[(banks[3], 65 * t) for t in range(4)], [(banks[4], 65 * t) for t in range(4)]]
                            for h in range(8):
                                ob = obsets[h % 2]

                                def post(h=h, obn=obn, ob=ob):
                                    oraw = orawp.next()
                                    for t in range(4):
                                        S.add("dve", (lambda t, oraw: lambda e: e.tensor_copy(
                                            out=oraw.t[:, t, 0:65], in_=ob[t][0].t[:, ob[t][1]:ob[t][1] + 65]))(t, oraw),
                                            reads=[ob[t][0].b], writes=[oraw.b])
                                    r = rp.next()
                                    S.add("dve", lambda e: e.reciprocal(out=r.t[:, 0:4], in_=oraw.t[:, :, 64]),
                                          reads=[oraw.b], writes=[r.b])
                                    for t in range(4):
                                        S.add("dve", (lambda t: lambda e: e.tensor_scalar(
                                            out=obn.t[:, t, h * 64:(h + 1) * 64], in0=oraw.t[:, t, 0:64],
                                            scalar1=r.t[:, t:t + 1], scalar2=None, op0=ALU.mult))(t),
                                            reads=[oraw.b, r.b], writes=[obn.b])

                                groups.append(dict(
                                    qT=qbT.t[0:96, h, :],
                                    kT=(lambda kc, h=h: KbT.t[0:96, h, kc * 128:(kc + 1) * 128]),
                                    v=(lambda kc, h=h: vB.t[:, kc, h * 65:(h + 1) * 65]),
                                    ncols=65, scale=96 ** -0.5, qkreads=[qbT.b, KbT.b], vreads=[vB.b], post=post, ob=ob))
                            attention(groups, spool, obanks, ptp, bg=(tail_bg + (bg or [])), every=1)
                            S.rec_begin()
                            S.rec_pad(8)
                            for t in range(4):
                                tb = tpool.next()[0]
                                tbf = bf(tb)
                                for c in range(4):
                                    S.add("pe", (lambda t, c, tbf, obn: lambda e: e.transpose(
                                        tbf[:, c * 128:(c + 1) * 128], obn.t[:, t, c * 128:(c + 1) * 128], ident.t[:]))(t, c, tbf, obn),
                                        reads=[obn.b, ident.b], writes=[tb.b])
                                q0 = qb * 512 + t * 128
                                S.add("dve", (lambda tbf, q0: lambda e: e.tensor_copy(
                                    out=catT.t[:, 4:8, q0:q0 + 128], in_=tbf[:, 0:512].rearrange("p (h q) -> p h q", h=4)))(tbf, q0),
                                    reads=[tb.b], writes=[catT.b])
                                S.rec_yield()
                            tail_bg = S.rec_end()
                        for g_ in tail_bg:
                            S.replay(g_)
                        XN_ENG[0] = "act"
                        S.pad_on = False
                        S.emit()

                with ExitStack() as pf:
                    acc = [T(None) for _ in range(16)]
                    acc_t = sb(pf, "acc", [128, 16, D], F32)
                    h2T = [T(None, KC) for _ in range(4)]
                    h2T_t = sb(pf, "h2T", [128, KC, S_LAT], BF16)

                    with ExitStack() as ph:
                        wout = T(sb(ph, "wout", [128, KC, D], BF16))
                        xpool = ring(ph, "xt", [128, D], F32, 2)
                        xnpool = ring(ph, "xn", [128, D], BF16, 2)
                        junk = T(sb(ph, "junk", [128, D], BF16))
                        stp = ring(ph, "st", [128, 4], F32, 4)
                        tmpp = ring(ph, "tmpo", [128, 512], F32, 2)
                        tpool = Ring([(banks[4], banks[5]), (banks[6], banks[7])])
                        ppool = Ring(banks[0:4])
                        S.add("pool", lambda e: e.dma_start(out=wout.t[:], in_=wout_d.rearrange("(c p) n -> p c n", p=128)),
                              writes=[wout.b], dma=True)

                        class HV:
                            pass

                        prev = None
                        for tt in range(16):
                            xt = xpool.next()
                            S.add("sp", (lambda xt, tt: lambda e: e.dma_start(
                                out=xt.t[:], in_=x_d[j, tt * 128:(tt + 1) * 128, :]))(xt, tt), writes=[xt.b], dma=True)
                            a = acc[tt]
                            for half in range(2):
                                bo = ppool.next()
                                for c in range(KC):
                                    S.add("pe", (lambda tt, half, c, bo: lambda e: e.matmul(
                                        bo.t[:, :], lhsT=catT.t[:, c, tt * 128:(tt + 1) * 128],
                                        rhs=wout.t[:, c, half * 512:(half + 1) * 512],
                                        start=(c == 0), stop=(c == KC - 1)))(tt, half, c, bo),
                                        reads=[catT.b, wout.b], writes=[bo.b])
                                tmp = tmpp.next()
                                S.add("dve", (lambda half, bo, tmp: lambda e: e.tensor_tensor(
                                    out=tmp.t[:], in0=bo.t[:, :], in1=G.t[:, j, 0, half * 512:(half + 1) * 512],
                                    op=ALU.mult))(half, bo, tmp), reads=[bo.b, G.b], writes=[tmp.b])
                                S.add("pool", (lambda tt, half, tmp, xt: lambda e: e.tensor_tensor(
                                    out=acc_t[:, tt, half * 512:(half + 1) * 512], in0=tmp.t[:],
                                    in1=xt.t[:, half * 512:(half + 1) * 512], op=ALU.add))(tt, half, tmp, xt),
                                    reads=[tmp.b, xt.b], writes=[a.b])
                            av = HV()
                            av.t = acc_t[:, tt, :]
                            av.b = a.b
                            xn_cur = norm_a((xpool, xnpool, junk, stp), av)
                            if PIPE_O and prev is not None:
                                trans_b(prev[0], prev[1] % 4, prev[2], j, 3, 2, tpool, ("dve", "act"))
                            hv = HV()
                            blk = tt // 4
                            hv.t = h2T_t[:, :, blk * 512:(blk + 1) * 512]
                            hv.rb = h2T[blk].rb
                            prev = (xn_cur, tt, hv)
                            if not PIPE_O:
                                trans_b(prev[0], prev[1] % 4, prev[2], j, 3, 2, tpool, ("dve", "act"))
                        if PIPE_O:
                            trans_b(prev[0], prev[1] % 4, prev[2], j, 3, 2, tpool, ("dve", "act"))
                        S.emit()

                    with ExitStack() as ph:
                        wg_views = [T(catT.t[:, 4 * i:4 * i + 4, :].rearrange("p a (b n) -> p (a b) n", n=1024), 2)
                                    for i in range(2)]
                        wgp = Ring(wg_views)
                        wop = ring(ph, "wo", [128, 4, D], BF16, 2)
                        actp = ring(ph, "actT", [128, 4, 512], BF16, 2)
                        sgp = ring(ph, "sg", [128, 512], F32, 3)
                        otp = ring(ph, "ot", [128, D], F32, 2)
                        junkf = T(sb(ph, "junkf", [128, D], BF16))
                        stp = ring(ph, "st", [128, 4], F32, 4)
                        gupool = Ring(banks[0:4])
                        fopool = Ring(banks[4:8])
                        wfi_src = wfi_d.rearrange("(c p) n -> p c n", p=128)
                        wfo_src = wfo_d.rearrange("(c p) n -> p c n", p=128)
                        ngroups = (NHC + 3) // 4
                        for g in range(ngroups):
                            c0 = g * 4
                            ng = min(4, NHC - c0)
                            wg = wgp.next()
                            wo = wop.next()
                            S.add("pool", (lambda wg, c0, ng: lambda e: e.dma_start(
                                out=wg.t[:, :, 0:ng * 128], in_=wfi_src[:, :, c0 * 128:(c0 + ng) * 128]))(wg, c0, ng),
                                writes=[wg.rb[0]], dma=True)
                            S.add("pool", (lambda wg, c0, ng: lambda e: e.dma_start(
                                out=wg.t[:, :, 512:512 + ng * 128],
                                in_=wfi_src[:, :, FFN_H + c0 * 128:FFN_H + (c0 + ng) * 128]))(wg, c0, ng),
                                writes=[wg.rb[1]], dma=True)
                            S.add("pool", (lambda wo, c0, ng: lambda e: e.dma_start(
                                out=wo.t[:, 0:ng, :], in_=wfo_src[:, c0:c0 + ng, :]))(wo, c0, ng),
                                writes=[wo.b], dma=True)
                            for ci in range(ng):
                                S.add("pool", (lambda wo, ci: lambda e: e.tensor_tensor(
                                    out=wo.t[:, ci, :], in0=wo.t[:, ci, :], in1=G.t[:, j, 1, :], op=ALU.mult))(wo, ci),
                                    reads=[wo.b, G.b], writes=[wo.b])
                            for tb4 in range(4):
                                actT = actp.next()
                                hb = h2T[tb4]
                                for ci in range(ng):
                                    bg = gupool.next()
                                    bu = gupool.next()
                                    for c in range(KC):
                                        S.add("pe", (lambda c, ci, bg, wg, tb4: lambda e: e.matmul(
                                            bg.t[:, :], lhsT=wg.t[:, c, ci * 128:(ci + 1) * 128],
                                            rhs=h2T_t[:, c, tb4 * 512:(tb4 + 1) * 512],
                                            start=(c == 0), stop=(c == KC - 1)))(c, ci, bg, wg, tb4),
                                            reads=[wg.rb[0], hb.rb[c]], writes=[bg.b])
                                    for c in range(KC):
                                        S.add("pe", (lambda c, ci, bu, wg, tb4: lambda e: e.matmul(
                                            bu.t[:, :], lhsT=wg.t[:, c, 512 + ci * 128:512 + (ci + 1) * 128],
                                            rhs=h2T_t[:, c, tb4 * 512:(tb4 + 1) * 512],
                                            start=(c == 0), stop=(c == KC - 1)))(c, ci, bu, wg, tb4),
                                            reads=[wg.rb[1], hb.rb[c]], writes=[bu.b])
                                    sg = sgp.next()
                                    S.add("act", (lambda sg, bg: lambda e: e.activation(
                                        out=sg.t[:], in_=bg.t[:, :], func=AF.Silu))(sg, bg),
                                        reads=[bg.b], writes=[sg.b])
                                    S.add("dve", (lambda ci, sg, bu, actT: lambda e: e.tensor_tensor(
                                        out=actT.t[:, ci, :], in0=bu.t[:, :], in1=sg.t[:], op=ALU.mult))(ci, sg, bu, actT),
                                        reads=[bu.b, sg.b], writes=[actT.b])
                                for t in range(4):
                                    tt = tb4 * 4 + t
                                    for half in range(2):
                                        bo = fopool.next()
                                        for ci in range(ng):
                                            S.add("pe", (lambda t, half, ci, bo, actT, wo, ng: lambda e: e.matmul(
                                                bo.t[:, :], lhsT=actT.t[:, ci, t * 128:(t + 1) * 128],
                                                rhs=wo.t[:, ci, half * 512:(half + 1) * 512],
                                                start=(ci == 0), stop=(ci == ng - 1)))(t, half, ci, bo, actT, wo, ng),
                                                reads=[actT.b, wo.b], writes=[bo.b])
                                        eng = "dve" if half == 0 else "dve"
                                        S.add(eng, (lambda tt, half, bo: lambda e: e.tensor_tensor(
                                            out=acc_t[:, tt, half * 512:(half + 1) * 512], in0=bo.t[:, :],
                                            in1=acc_t[:, tt, half * 512:(half + 1) * 512], op=ALU.add))(tt, half, bo),
                                            reads=[bo.b, acc[tt].b], writes=[acc[tt].b])
                        for tt in range(16):
                            s = stp.next()
                            ot = otp.next()
                            S.add("dve", (lambda tt, s: lambda e: e.scalar_tensor_tensor(
                                out=junkf.t[:], in0=acc_t[:, tt, :], scalar=1.0, in1=acc_t[:, tt, :],
                                op0=ALU.mult, op1=ALU.mult, accum_out=s.t[:, 0:1]))(tt, s),
                                reads=[acc[tt].b], writes=[s.b])
                            S.add("dve", (lambda s: lambda e: e.tensor_scalar(
                                out=s.t[:, 1:2], in0=s.t[:, 0:1], scalar1=1.0 / D, scalar2=EPS,
                                op0=ALU.mult, op1=ALU.add))(s), reads=[s.b], writes=[s.b])
                            S.add("pool", (lambda s: lambda e: e.tensor_tensor(
                                out=s.t[:, 2:3], in0=s.t[:, 1:2], in1=neghalf.t[:, 0:1], op=ALU.pow))(s),
                                reads=[s.b, neghalf.b], writes=[s.b])
                            S.add("dve", (lambda tt, s, ot: lambda e: e.scalar_tensor_tensor(
                                out=ot.t[:], in0=acc_t[:, tt, :], scalar=s.t[:, 2:3], in1=fnorm_bc.t[:],
                                op0=ALU.mult, op1=ALU.mult))(tt, s, ot),
                                reads=[acc[tt].b, s.b, fnorm_bc.b], writes=[ot.b])
                            S.add("sp", (lambda tt, ot: lambda e: e.dma_start(
                                out=y_d[j, tt * 128:(tt + 1) * 128, :], in_=ot.t[:]))(tt, ot),
                                reads=[ot.b], dma=True)
                        S.emit()
    return nc


def _rope_tables():
    t = np.arange(S_LAT)
    row = (t // GRID_W).astype(np.float64)
    col = (t % GRID_W).astype(np.float64)
    tabA = np.zeros((128, 2, S_LAT), np.float32)
    for p in range(128):
        d = p % 64
        pos = row if d < 32 else col
        inv = THETA ** (-(d % 16) / 16.0)
        ang = pos * inv
        sign = -1.0 if (d % 32) < 16 else 1.0
        tabA[p, 0] = np.cos(ang)
        tabA[p, 1] = sign * np.sin(ang)
    tabB = np.zeros((128, 2, S_LAT), np.float32)
    for p in list(range(0, 32)) + list(range(64, 96)):
        d = p % 32
        pos = row if d < 16 else col
        inv = THETA ** (-(d % 8) / 8.0)
        ang = pos * inv
        sign = -1.0 if (d % 16) < 8 else 1.0
        tabB[p, 0] = np.cos(ang)
        tabB[p, 1] = sign * np.sin(ang)
    return tabA, tabB


_NC_CACHE = {}


def kernel(x, c, ctx, c_ctx, w_ada, b_ada, w_in, q_a_norm, kv_a_norm, w_q_up, w_kv_up,
           diff_lambda, diff_subln, w_out, w_ffn_in, w_ffn_out, final_norm):
    f32 = np.float32
    x = np.asarray(x, f32)
    c = np.asarray(c, f32)
    ctx = np.asarray(ctx, f32)
    c_ctx = np.asarray(c_ctx, f32)
    w_ada0 = np.ascontiguousarray(np.asarray(w_ada, f32)[0])
    b_ada0 = np.asarray(b_ada, f32)[0]
    w_in0 = np.asarray(w_in, f32)[0]
    w_q_up0 = np.asarray(w_q_up, f32)[0]
    w_kv_up0 = np.asarray(w_kv_up, f32)[0]

    perm64 = np.array([i + 16 if (i % 32) < 16 else i - 16 for i in range(64)])
    perm32 = np.array([i + 8 if (i % 16) < 8 else i - 8 for i in range(32)])
    permA = np.concatenate([64 * b + perm64 for b in range(8)])
    qA, kA, vA = w_in0[:, 0:512], w_in0[:, 512:1024], w_in0[:, 1024:1536]
    cq, ckv, kr = w_in0[:, 1536:1792], w_in0[:, 1792:1920], w_in0[:, 1920:1952]
    w_in_d = np.ascontiguousarray(np.concatenate([kA, kA[:, permA], vA, qA, qA[:, permA]], axis=1))
    w_in_m = np.ascontiguousarray(np.concatenate([ckv, kr, kr[:, perm32], cq], axis=1))
    rope_cols = np.concatenate([96 * h + 64 + perm32 for h in range(8)])
    w_qupp = np.ascontiguousarray(w_q_up0[:, rope_cols])
    kcols = np.concatenate([128 * h + np.arange(64) for h in range(8)])
    vcols = np.concatenate([128 * h + 64 + np.arange(64) for h in range(8)])
    w_kvk = np.ascontiguousarray(w_kv_up0[:, kcols])
    w_kvv = np.ascontiguousarray(w_kv_up0[:, vcols])
    tabA, tabB = _rope_tables()

    bigs = np.empty((128, 3072), f32)
    bigs[:, 0:1024] = b_ada0[None, 2048:3072]
    bigs[:, 1024:2048] = b_ada0[None, 5120:6144]
    bigs[:, 2048:3072] = np.asarray(final_norm, f32)[None, :]

    def fm(v):
        return np.asarray(v, f32).reshape(-1, 128).T

    shared = {
        "bigs": bigs, "w_ada": w_ada0, "w_in_d": w_in_d, "w_in_m": w_in_m,
        "w_qup": np.ascontiguousarray(w_q_up0), "w_qupp": w_qupp, "w_kvk": w_kvk, "w_kvv": w_kvv,
        "w_out": np.ascontiguousarray(np.asarray(w_out, f32)[0]),
        "w_ffn_in": np.ascontiguousarray(np.asarray(w_ffn_in, f32)[0]),
        "w_ffn_out": np.ascontiguousarray(np.asarray(w_ffn_out, f32)[0]),
        "tab_a": tabA, "tab_b": tabB,
    }
    in_maps = []
    for i in range(N_CORES):
        sm = np.zeros((128, 320), f32)
        cvecs = [c[2 * i], c[2 * i + 1], c_ctx]
        for jv, v in enumerate(cvecs):
            sm[:, 0 + jv:24:3] = fm(v)
        for ki, kind in enumerate((0, 1, 3, 4)):
            sm[:, 24 + ki * 8:24 + (ki + 1) * 8] = fm(b_ada0[kind * 1024:(kind + 1) * 1024])
        sm[:, 56:312] = np.asarray(diff_lambda, f32)[0].reshape(1, 256)
        sm[:, 312] = np.asarray(diff_subln, f32)[0]
        sm[:, 313:315] = fm(np.asarray(q_a_norm, f32)[0])
        sm[:, 315] = np.asarray(kv_a_norm, f32)[0]
        m = dict(shared)
        m["x"] = np.ascontiguousarray(x[2 * i:2 * i + 2])
        m["ctx"] = np.ascontiguousarray(ctx[2 * i:2 * i + 2])
        m["smalls"] = sm
        in_maps.append(m)

    if "nc" not in _NC_CACHE:
        _NC_CACHE["nc"] = build_program()
    nc = _NC_CACHE["nc"]
    res = run_bass_kernel_spmd(nc, in_maps, core_ids=list(range(N_CORES)))
    out = np.concatenate([np.asarray(r["y"], f32) for r in res.results], axis=0)
    return out
```
